# Optimizing a Trainium2 kernel written in Bass

```python
import jax, jax.numpy as jnp
from jax import lax
import numpy as np

D_MODEL = 1024
BATCH = 8
SEQ = 4096
DEPTH = 1

MLA_HEADS = 8
Q_LORA_RANK = 256
KV_LORA_RANK = 256
QK_NOPE_DIM = 64
QK_ROPE_DIM = 32
V_HEAD_DIM = 64
ROPE_THETA = 10000.0
Q_BLOCK = 128
SWA_HEADS = 8
SWA_KV_HEADS = 2
SWA_HEAD_DIM = 64
WINDOW = 128
BAND_BLOCK = 128
MIX_WIDTH = MLA_HEADS * V_HEAD_DIM + SWA_HEADS * SWA_HEAD_DIM
IN_SPLITS = (Q_LORA_RANK, KV_LORA_RANK, QK_ROPE_DIM,
             SWA_HEADS * SWA_HEAD_DIM, SWA_KV_HEADS * SWA_HEAD_DIM, SWA_KV_HEADS * SWA_HEAD_DIM)
IN_WIDTH = sum(IN_SPLITS)
D_FF = 2816
CONV_WIDTH = 3
LN_EPS = 1e-5
RMS_EPS = 1e-6
DEEPNORM_ALPHA = (2.0 * DEPTH) ** 0.25
DEEPNORM_BETA = (8.0 * DEPTH) ** -0.25
NEG_BIG = -1e30

kernel_name = "hybrid_mla_swa_convffn_deepnorm"


def _layer_norm(x, g, b):
    xf = x.astype(jnp.float32)
    mu = jnp.mean(xf, axis=-1, keepdims=True)
    var = jnp.mean(jnp.square(xf - mu), axis=-1, keepdims=True)
    return ((xf - mu) * lax.rsqrt(var + LN_EPS) * g.astype(jnp.float32) + b.astype(jnp.float32)).astype(x.dtype)


def _rms_norm(x, g):
    xf = x.astype(jnp.float32)
    r = lax.rsqrt(jnp.mean(jnp.square(xf), axis=-1, keepdims=True) + RMS_EPS)
    return (xf * r * g.astype(jnp.float32)).astype(x.dtype)


def _rope_cos_sin(positions):
    inv_freq = ROPE_THETA ** (-jnp.arange(0, QK_ROPE_DIM, 2, dtype=jnp.float32) / QK_ROPE_DIM)
    ang = positions.astype(jnp.float32)[..., None] * inv_freq
    return jnp.cos(ang), jnp.sin(ang)


def _rotate(x, cos, sin):
    xf = x.astype(jnp.float32)
    x1, x2 = jnp.split(xf, 2, axis=-1)
    return jnp.concatenate([x1 * cos - x2 * sin, x1 * sin + x2 * cos], axis=-1).astype(x.dtype)


def _alibi_slopes(n_heads):
    return 2.0 ** (-8.0 * (np.arange(n_heads, dtype=np.float32) + 1.0) / n_heads)


def _mla_attention(q, k, v):
    B, S, H, DQ = q.shape
    nq = S // Q_BLOCK
    qb = q.reshape(B, nq, Q_BLOCK, H, DQ).transpose(1, 0, 2, 3, 4)
    scale = DQ ** -0.5

    def one_block(q_blk):
        s = jnp.einsum('bqhd,bkhd->bhqk', q_blk, k).astype(jnp.float32) * scale
        p = jax.nn.softmax(s, axis=-1).astype(v.dtype)
        return jnp.einsum('bhqk,bkhd->bqhd', p, v)

    o = lax.map(one_block, qb)
    return o.transpose(1, 0, 2, 3, 4).reshape(B, S, H * v.shape[-1])


def _band(t):
    B, S = t.shape[0], t.shape[1]
    nb = S // BAND_BLOCK
    pad = [(0, 0), (BAND_BLOCK, BAND_BLOCK)] + [(0, 0)] * (t.ndim - 2)
    tp = jnp.pad(t, pad).reshape((B, nb + 2, BAND_BLOCK) + t.shape[2:])
    return jnp.concatenate([tp[:, :-2], tp[:, 1:-1], tp[:, 2:]], axis=2)


def _window_gqa_attention(q, k, v, positions, sinks):
    B, S, H, D = q.shape
    KVH = k.shape[2]
    G = H // KVH
    nb = S // BAND_BLOCK
    qb = q.reshape(B, nb, BAND_BLOCK, KVH, G, D)
    kb, vb = _band(k), _band(v)
    pk = _band(positions)
    pq = positions.reshape(B, nb, BAND_BLOCK)
    dist = jnp.abs(pq[..., :, None] - pk[..., None, :]).astype(jnp.float32)
    a = jnp.arange(BAND_BLOCK)[:, None]
    c = jnp.arange(3 * BAND_BLOCK)[None, :]
    rel = c - BAND_BLOCK - a
    key_idx = jnp.arange(nb)[:, None, None] * BAND_BLOCK - BAND_BLOCK + c[None]
    mask = (jnp.abs(rel)[None] <= WINDOW) & (key_idx >= 0) & (key_idx < S)

    s = jnp.einsum('bnqkgd,bnckd->bnkgqc', qb, kb).astype(jnp.float32) * (D ** -0.5)
    slopes = jnp.asarray(_alibi_slopes(H)).reshape(KVH, G)[:, :, None, None]
    s = s - slopes * dist[:, :, None, None]
    s = jnp.where(mask[None, :, None, None], s, NEG_BIG)
    sk = sinks.astype(jnp.float32).reshape(KVH, G)[:, :, None, None]
    m = jnp.maximum(jnp.max(s, axis=-1, keepdims=True), sk)
    e = jnp.exp(s - m)
    p = e / (jnp.sum(e, axis=-1, keepdims=True) + jnp.exp(sk - m))
    o = jnp.einsum('bnkgqc,bnckd->bnqkgd', p.astype(v.dtype), vb)
    return o.reshape(B, S, H * D)


def _conv_ffn(x, w_up, conv_w, conv_b, w_down):
    h = jnp.einsum('bsd,df->bsf', x, w_up)
    h = lax.conv_general_dilated(
        h, conv_w, window_strides=(1,), padding=((CONV_WIDTH // 2, CONV_WIDTH // 2),),
        dimension_numbers=('NWC', 'WIO', 'NWC'), feature_group_count=h.shape[-1]) + conv_b
    g, u = jnp.split(h, 2, axis=-1)
    return jnp.einsum('bsf,fd->bsd', jax.nn.gelu(g, approximate=False) * u, w_down)


def setup_inputs(seed: int = 0) -> dict:
    key = jax.random.key(seed)
    ks = jax.random.split(key, 20)
    f32 = jnp.float32

    def nrm(k, shape, scale):
        return jax.random.normal(k, shape, f32) * scale

    x = jax.random.normal(ks[0], (BATCH, SEQ, D_MODEL), f32)
    positions = jnp.broadcast_to(jnp.arange(SEQ, dtype=jnp.int32)[None, :], (BATCH, SEQ))
    return {
        "x": x,
        "positions": positions,
        "w_in": nrm(ks[1], (D_MODEL, IN_WIDTH), D_MODEL ** -0.5),
        "q_norm_g": 1.0 + nrm(ks[2], (Q_LORA_RANK,), 0.02),
        "w_q_b": nrm(ks[3], (Q_LORA_RANK, MLA_HEADS * (QK_NOPE_DIM + QK_ROPE_DIM)), Q_LORA_RANK ** -0.5),
        "kv_norm_g": 1.0 + nrm(ks[4], (KV_LORA_RANK,), 0.02),
        "w_kv_b": nrm(ks[5], (KV_LORA_RANK, MLA_HEADS * (QK_NOPE_DIM + V_HEAD_DIM)), KV_LORA_RANK ** -0.5),
        "swa_sinks": nrm(ks[6], (SWA_HEADS,), 0.5),
        "w_o": nrm(ks[7], (MIX_WIDTH, D_MODEL), MIX_WIDTH ** -0.5) * DEEPNORM_BETA,
        "ln1_g": 1.0 + nrm(ks[8], (D_MODEL,), 0.02),
        "ln1_b": nrm(ks[9], (D_MODEL,), 0.02),
        "w_up": nrm(ks[10], (D_MODEL, 2 * D_FF), D_MODEL ** -0.5),
        "conv_w": nrm(ks[11], (CONV_WIDTH, 1, 2 * D_FF), CONV_WIDTH ** -0.5),
        "conv_b": nrm(ks[12], (2 * D_FF,), 0.02),
        "w_down": nrm(ks[13], (D_FF, D_MODEL), D_FF ** -0.5) * DEEPNORM_BETA,
        "ln2_g": 1.0 + nrm(ks[14], (D_MODEL,), 0.02),
        "ln2_b": nrm(ks[15], (D_MODEL,), 0.02),
    }


def reference(x, positions, w_in, q_norm_g, w_q_b, kv_norm_g, w_kv_b, swa_sinks, w_o,
              ln1_g, ln1_b, w_up, conv_w, conv_b, w_down, ln2_g, ln2_b):
    B, S, _ = x.shape
    cos, sin = _rope_cos_sin(positions)
    for _layer in range(DEPTH):
        proj = jnp.einsum('bsd,df->bsf', x, w_in)
        offs = np.cumsum(IN_SPLITS)[:-1].tolist()
        c_q, c_kv, k_rope, q_s, k_s, v_s = jnp.split(proj, offs, axis=-1)

        q = jnp.einsum('bsr,rf->bsf', _rms_norm(c_q, q_norm_g), w_q_b)
        q = q.reshape(B, S, MLA_HEADS, QK_NOPE_DIM + QK_ROPE_DIM)
        q_nope, q_rope = q[..., :QK_NOPE_DIM], q[..., QK_NOPE_DIM:]
        q_rope = _rotate(q_rope, cos[:, :, None, :], sin[:, :, None, :])
        kv = jnp.einsum('bsr,rf->bsf', _rms_norm(c_kv, kv_norm_g), w_kv_b)
        kv = kv.reshape(B, S, MLA_HEADS, QK_NOPE_DIM + V_HEAD_DIM)
        k_nope, v_mla = kv[..., :QK_NOPE_DIM], kv[..., QK_NOPE_DIM:]
        k_rope = _rotate(k_rope, cos, sin)
        q_mla = jnp.concatenate([q_nope, q_rope], axis=-1)
        k_mla = jnp.concatenate(
            [k_nope, jnp.broadcast_to(k_rope[:, :, None, :], (B, S, MLA_HEADS, QK_ROPE_DIM))], axis=-1)
        o_mla = _mla_attention(q_mla, k_mla, v_mla)

        o_swa = _window_gqa_attention(
            q_s.reshape(B, S, SWA_HEADS, SWA_HEAD_DIM),
            k_s.reshape(B, S, SWA_KV_HEADS, SWA_HEAD_DIM),
            v_s.reshape(B, S, SWA_KV_HEADS, SWA_HEAD_DIM),
            positions, swa_sinks)

        mix = jnp.einsum('bsf,fd->bsd', jnp.concatenate([o_mla, o_swa], axis=-1), w_o)
        x = _layer_norm(DEEPNORM_ALPHA * x + mix, ln1_g, ln1_b)

        ff = _conv_ffn(x, w_up, conv_w, conv_b, w_down)
        x = _layer_norm(DEEPNORM_ALPHA * x + ff, ln2_g, ln2_b)
    return x
```

```python
import contextlib
import math
import numpy as np
import concourse.bass as bass
import concourse.mybir as mybir
from concourse.bass_utils import run_bass_kernel_spmd

F32 = mybir.dt.float32
BF16 = mybir.dt.bfloat16
I32 = mybir.dt.int32
AF = mybir.ActivationFunctionType
ALU = mybir.AluOpType

D = 1024
DFF = 2816
NJ = 22
IN_W = 1312
ALPHA = 2.0 ** 0.25
LN_EPS = 1e-5
RMS_EPS = 1e-6
BIGM = 32768.0
ENGS = ("pe", "act", "dve", "pool", "sp")


class Op:
    __slots__ = ("eng", "fn", "deps", "needed", "seq", "key", "is_dma", "idx")

    def __init__(self, eng, fn, key, is_dma):
        self.eng = eng
        self.fn = fn
        self.deps = {}
        self.needed = False
        self.seq = 0
        self.key = key
        self.is_dma = is_dma
        self.idx = 0


class Buf:
    __slots__ = ("name", "writers", "readers")

    def __init__(self, name=""):
        self.name = name
        self.writers = {}
        self.readers = {}


class Sched:
    def __init__(self, nc):
        self.nc = nc
        self.ops = {e: [] for e in ENGS}
        self.stream_cnt = {}
        self.last = {}
        self.bar = {}

    def _add_dep(self, op, d):
        if d is None or d is op:
            return
        if d.key == op.key and op.eng == "pe" and not op.is_dma:
            return
        cur = op.deps.get(d.key)
        if cur is None or d.idx > cur.idx:
            op.deps[d.key] = d

    def barrier(self):
        self.bar = dict(self.last)

    def op(self, eng, fn, reads=(), writes=(), deps=(), stream=None):
        is_dma = stream is not None
        key = stream if is_dma else eng
        o = Op(eng, fn, key, is_dma)
        if is_dma:
            self.stream_cnt[stream] = self.stream_cnt.get(stream, 0) + 1
            o.idx = self.stream_cnt[stream]
        else:
            o.idx = len(self.ops[eng]) + 1
        for d in self.bar.values():
            self._add_dep(o, d)
        for d in deps:
            self._add_dep(o, d)
        for b in reads:
            for w in b.writers.values():
                self._add_dep(o, w)
        for b in writes:
            for r in b.readers.values():
                self._add_dep(o, r)
            for w in b.writers.values():
                self._add_dep(o, w)
        for b in reads:
            b.readers[key] = o
        for b in writes:
            if b.readers:
                b.readers = {}
                b.writers = {}
            b.writers[key] = o
        for d in o.deps.values():
            d.needed = True
        self.ops[eng].append(o)
        self.last[key] = o
        return o

    def pe(self, fn, **kw):
        return self.op("pe", fn, **kw)

    def act(self, fn, **kw):
        return self.op("act", fn, **kw)

    def dve(self, fn, **kw):
        return self.op("dve", fn, **kw)

    def pool(self, fn, **kw):
        return self.op("pool", fn, **kw)

    def dma(self, stream, fn, eng="sp", **kw):
        return self.op(eng, fn, stream=stream, **kw)

    def emit(self, final_waits=()):
        nc = self.nc
        for e in ENGS:
            n = 0
            for o in self.ops[e]:
                if not o.is_dma and o.needed:
                    n += 1
                    o.seq = n
        with contextlib.ExitStack() as st:
            sems = {}
            for e in ENGS:
                sems[e] = st.enter_context(nc.semaphore("s_" + e))
            for s in self.stream_cnt:
                sems[s] = st.enter_context(nc.semaphore("d_" + s))
            block = st.enter_context(nc.Block())

            def run(e, engobj):
                waited = {}

                def wait(d):
                    val = 16 * d.idx if d.is_dma else d.seq
                    if waited.get(d.key, 0) < val:
                        engobj.wait_ge(sems[d.key], val)
                        waited[d.key] = val

                for o in self.ops[e]:
                    for d in o.deps.values():
                        wait(d)
                    ins = o.fn(engobj)
                    if o.is_dma:
                        ins.then_inc(sems[o.key], 16)
                    elif o.needed:
                        ins.then_inc(sems[o.key], 1)
                if e == "sp":
                    for d in final_waits:
                        wait(d)

            @block.tensor
            def _(e):
                run("pe", e)

            @block.scalar
            def _(e):
                run("act", e)

            @block.vector
            def _(e):
                run("dve", e)

            @block.gpsimd
            def _(e):
                run("pool", e)

            @block.sync
            def _(e):
                run("sp", e)


class _Done(Exception):
    pass


class Arena:
    def __init__(self, ap, n):
        self.ap = ap
        self.n = n
        self.off = 0

    def alloc(self, cols, dt):
        units = cols * (2 if dt in (F32, I32) else 1)
        units = (units + 15) // 16 * 16
        a = self.ap[:, self.off:self.off + units]
        self.off += units
        assert self.off <= self.n, f"arena overflow {self.off} > {self.n}"
        if dt != BF16:
            a = a.bitcast(dt)
        return a[:, 0:cols]

    def mark(self):
        return self.off

    def release(self, m):
        self.off = m


def v3(ap, a):
    return ap.rearrange("p (a b) -> p a b", a=a)


def build(S, dbg=None, stop=None):
    NCH = S // 512
    NT = S // 128
    nc = bass.Bass("TRN2", target_bir_lowering=False)

    def din(name, shape, dt):
        return nc.dram_tensor(name, shape, dt, kind="ExternalInput").ap()

    x_d = din("x", [S, D], F32)
    pos_d = din("pos", [1, S], I32)
    posr_d = din("posr", [NT, 128], I32)
    w_in_d = din("w_in", [D, IN_W], F32)
    w_insw_d = din("w_insw", [D, 96], F32)
    w_qb_d = din("w_qb", [256, 768], F32)
    w_qbsw_d = din("w_qbsw", [256, 768], F32)
    w_kvb_d = din("w_kvb", [256, 1024], F32)
    w_o_d = din("w_o", [D, D], F32)
    w_up_d = din("w_up", [D, 2 * DFF], F32)
    w_dn_d = din("w_down", [DFF, D], F32)
    graw_d = din("graw", [4, 128], F32)
    sinks_d = din("sinks", [1, 8], F32)
    ln_d = din("lnp", [4, D], F32)
    convp_d = din("convp", [4, 44, 128], F32)
    ident_d = din("ident", [128, 128], F32)
    cst_d = din("cst", [128, 8], F32)
    masks_d = din("masks", [2, 128, 128], F32)
    slopes_d = din("slopes", [128, 8], F32)
    out_d = nc.dram_tensor("out", [S, D], F32, kind="ExternalOutput").ap()
    wup_s = nc.dram_tensor("wup_s", [NJ, 128, 8, 256], BF16).ap()
    wdn_s = nc.dram_tensor("wdn_s", [DFF, D], BF16).ap()
    dbg_d = None
    if dbg is not None:
        dbg_d = nc.dram_tensor("dbg", [128, dbg[1]], dbg[2], kind="ExternalOutput").ap()

    st = contextlib.ExitStack()
    with st:
        ARENA_N = 106000
        arena_t = st.enter_context(nc.sbuf_tensor("arena", [128, ARENA_N], BF16))
        PS = st.enter_context(nc.psum_tensor("ps", [128, 4096], F32))
        A = Arena(arena_t, ARENA_N)
        S_ = Sched(nc)
        PB = [Buf(f"bank{i}") for i in range(8)]

        def bank(i, n=1):
            return PS[:, i * 512:(i + n) * 512]

        def bankb(i):
            return PS[:, i * 512:(i + 1) * 512].bitcast(BF16)

        def mm(out, lhsT, rhs, start, stop, **kw):
            return S_.pe(lambda e: e.matmul(out, lhsT=lhsT, rhs=rhs, start=start, stop=stop), **kw)

        identb = A.alloc(128, BF16)
        onesb = A.alloc(128, BF16)
        ident32 = A.alloc(128, F32)
        cst = A.alloc(8, F32)
        slp = A.alloc(8, F32)
        gcol = A.alloc(4, F32)
        esink = A.alloc(8, F32)
        convp = A.alloc(4 * 44, F32)
        OTs = A.alloc(4 * S, BF16)
        OTs3 = v3(OTs, 4)
        m_all = A.mark()
        OTM_N = max(4 * S, 16384)
        OTm_full = A.alloc(OTM_N, BF16)
        OTm = OTm_full[:, 0:4 * S]
        OTm3 = v3(OTm, 4)
        m_p3 = A.mark()
        A.release(m_all)
        A.off = m_p3
        cqn = A.alloc(2 * S, BF16)
        ckvn = A.alloc(2 * S, BF16)
        cqn3, ckvn3 = v3(cqn, 2), v3(ckvn, 2)
        krope = A.alloc(S, BF16)
        ropeC = A.alloc(S, BF16)
        ropeS = A.alloc(S, BF16)
        w_qb = A.alloc(2 * 768, BF16)
        w_qbsw = A.alloc(2 * 768, BF16)
        w_kvb = A.alloc(2 * 1024, BF16)
        w_qb3, w_qbsw3, w_kvb3 = v3(w_qb, 2), v3(w_qbsw, 2), v3(w_kvb, 2)
        m_p12 = A.mark()

        B_const = Buf("const")
        dbg_srcs = dict(OTs=OTs, OTm=OTm, cqn=cqn, ckvn=ckvn, krope=krope, ropeC=ropeC, ropeS=ropeS)

        def finish(final_ops):
            finals = list(final_ops)
            if dbg is not None:
                S_.barrier()
                src = dbg_srcs[dbg[0]]
                finals.append(S_.dma("dbg", lambda e: e.dma_start(out=dbg_d, in_=src)))
            S_.emit(final_waits=finals)

        A2 = Arena(OTm_full, OTM_N)
        stg32 = [A2.alloc(512, F32) for _ in range(2)]
        stg16 = [A2.alloc(512, BF16) for _ in range(2)]
        Bstg32 = [Buf() for _ in range(2)]
        Bstg16 = [Buf() for _ in range(2)]
        masks = A2.alloc(2 * 128, F32)
        masks3 = v3(masks, 2)
        dg = A2.alloc(8 * 128, BF16)
        dg3 = v3(dg, 8)
        pcol = A2.alloc(NT, F32)
        npcol = A2.alloc(NT, F32)
        w_in = A2.alloc(8 * IN_W, BF16)
        w_in3 = v3(w_in, 8)
        w_insw = A2.alloc(8 * 96, BF16)
        w_insw3 = v3(w_insw, 8)
        m_p1 = A.mark()

        S_.dma("k1", lambda e: e.dma_start(out=ident32, in_=ident_d), writes=[B_const])
        S_.dma("k2", lambda e: e.dma_start(out=cst, in_=cst_d), writes=[B_const])
        S_.dma("k3", lambda e: e.dma_start(out=slp, in_=slopes_d), writes=[B_const])
        S_.dma("k4", lambda e: e.dma_start(out=masks3, in_=masks_d.rearrange("m p f -> p m f")), writes=[B_const])
        S_.dma("k5", lambda e: e.dma_start(out=esink, in_=sinks_d.partition_broadcast(128)), writes=[B_const])
        S_.dve(lambda e: e.tensor_copy(out=identb, in_=ident32), reads=[B_const], writes=[B_const])
        S_.dve(lambda e: e.memset(onesb, 1.0), writes=[B_const])
        for g in range(2):
            for slot in range(4):
                par, j = slot // 2, slot % 2
                h = 4 * g + 2 * j + par
                S_.dve(lambda e, h=h, k=g * 4 + slot: e.tensor_scalar(out=dg3[:, k, :], in0=ident32, scalar1=slp[:, h:h + 1], scalar2=None,
                                                                      op0=ALU.mult), reads=[B_const], writes=[B_const])
        S_.act(lambda e: e.activation(out=esink, in_=esink, func=AF.Exp), reads=[B_const], writes=[B_const])

        if stop == 0.1:
            finish([])
            return nc
        tmpA = A.alloc(4 * 128, F32)
        tmpA3 = v3(tmpA, 4)
        tmpI = A.alloc(128, I32)
        tmpF = A.alloc(128, F32)
        Btmp = Buf()
        S_.dma("k6", lambda e: e.dma_start(out=tmpA3[0:44, :, :], in_=convp_d.rearrange("k c p -> c k p")), writes=[Btmp])
        for k in range(4):
            S_.pe(lambda e, k=k: e.transpose(out=bank(0)[:, k * 44:(k + 1) * 44], in_=tmpA3[0:44, k, :], identity=ident32[0:44, 0:44]),
                  reads=[Btmp, B_const], writes=[PB[0]])
        S_.dve(lambda e: e.tensor_copy(out=convp, in_=bank(0)[:, 0:176]), reads=[PB[0]], writes=[B_const])
        tmpG = A.alloc(128, F32)
        BtmpG = Buf()
        S_.dma("k7", lambda e: e.dma_start(out=tmpG[0:4, :], in_=graw_d), writes=[BtmpG])
        S_.pe(lambda e: e.transpose(out=bank(1)[:, 0:4], in_=tmpG[0:4, :], identity=ident32[0:4, 0:4]),
              reads=[BtmpG, B_const], writes=[PB[1]])
        S_.dve(lambda e: e.tensor_copy(out=gcol, in_=bank(1)[:, 0:4]), reads=[PB[1]], writes=[B_const])
        BtmpP = Buf()
        S_.dma("k8", lambda e: e.dma_start(out=tmpI[0:NT, :], in_=posr_d), writes=[BtmpP])
        S_.dve(lambda e: e.tensor_copy(out=tmpF[0:NT, :], in_=tmpI[0:NT, :]), reads=[BtmpP], writes=[BtmpP])
        S_.pe(lambda e: e.transpose(out=bank(2)[:, 0:NT], in_=tmpF[0:NT, :], identity=ident32[0:NT, 0:NT]),
              reads=[BtmpP, B_const], writes=[PB[2]])
        S_.dve(lambda e: e.tensor_copy(out=pcol, in_=bank(2)[:, 0:NT]), reads=[PB[2]], writes=[B_const])
        S_.dve(lambda e: e.tensor_scalar(out=npcol, in0=bank(2)[:, 0:NT], scalar1=-1.0, scalar2=None, op0=ALU.mult), reads=[PB[2]], writes=[B_const])

        if stop == 0.2:
            finish([])
            return nc
        r_i = A.alloc(512, I32)
        r_f = A.alloc(512, F32)
        r_t = A.alloc(512, F32)
        r_u = A.alloc(512, F32)
        r_k = A.alloc(512, I32)
        Br = Buf()
        PR = slice(64, 96)
        for c in range(NCH):
            cs = slice(c * 512, (c + 1) * 512)
            S_.dma("rp", lambda e, cs=cs: e.dma_start(out=r_i[PR, :], in_=pos_d[0:1, cs].partition_broadcast(32)), writes=[Br], eng="pool")
            S_.dve(lambda e: e.tensor_copy(out=r_f[PR, :], in_=r_i[PR, :]), reads=[Br], writes=[Br])
            for which in range(2):
                S_.dve(lambda e, which=which: e.tensor_scalar(out=r_t[PR, :], in0=r_f[PR, :], scalar1=cst[PR, 0:1],
                                                              scalar2=(0.25 if which == 0 else 0.0), op0=ALU.mult, op1=ALU.add),
                       reads=[Br, B_const], writes=[Br])
                S_.dve(lambda e: e.tensor_copy(out=r_k[PR, :], in_=r_t[PR, :]), reads=[Br], writes=[Br])
                S_.dve(lambda e: e.tensor_copy(out=r_u[PR, :], in_=r_k[PR, :]), reads=[Br], writes=[Br])
                S_.dve(lambda e: e.tensor_tensor(out=r_t[PR, :], in0=r_t[PR, :], in1=r_u[PR, :], op=ALU.subtract), reads=[Br], writes=[Br])
                if which == 0:
                    S_.act(lambda e, cs=cs: e.activation(out=ropeC[PR, cs], in_=r_t[PR, :], func=AF.Sin, scale=2.0 * math.pi),
                           reads=[Br], writes=[Br])
                else:
                    S_.act(lambda e, cs=cs: e.activation(out=ropeS[PR, cs], in_=r_t[PR, :], func=AF.Sin, scale=cst[PR, 1:2]),
                           reads=[Br, B_const], writes=[Br])

        if stop == 0.3:
            finish([])
            return nc
        cast_i = [0]

        def load_cast(dst_ap_fn, src_ap, ncols, dst_buf, stg=None, Bstg=None):
            stg = stg32 if stg is None else stg
            Bstg = Bstg32 if Bstg is None else Bstg
            for c0 in range(0, ncols, 512):
                c1 = min(ncols, c0 + 512)
                i = cast_i[0] % 2
                cast_i[0] += 1
                S_.dma(f"wl{i}", lambda e, i=i, c0=c0, c1=c1: e.dma_start(out=stg[i][:, 0:c1 - c0], in_=src_ap[:, c0:c1]),
                       writes=[Bstg[i]])
                if i == 0:
                    S_.dve(lambda e, i=i, c0=c0, c1=c1: e.tensor_copy(out=dst_ap_fn(c0, c1), in_=stg[i][:, 0:c1 - c0]),
                           reads=[Bstg[i]], writes=[dst_buf])
                else:
                    S_.act(lambda e, i=i, c0=c0, c1=c1: e.copy(out=dst_ap_fn(c0, c1), in_=stg[i][:, 0:c1 - c0]),
                           reads=[Bstg[i]], writes=[dst_buf])

        Bw = Buf("weights")
        for kc in range(8):
            load_cast(lambda c0, c1, kc=kc: w_in3[:, kc, c0:c1], w_in_d[kc * 128:(kc + 1) * 128, :], IN_W, Bw)
            load_cast(lambda c0, c1, kc=kc: w_insw3[:, kc, c0:c1], w_insw_d[kc * 128:(kc + 1) * 128, :], 96, Bw)
        for kc in range(2):
            load_cast(lambda c0, c1, kc=kc: w_qb3[:, kc, c0:c1], w_qb_d[kc * 128:(kc + 1) * 128, :], 768, Bw)
            load_cast(lambda c0, c1, kc=kc: w_qbsw3[:, kc, c0:c1], w_qbsw_d[kc * 128:(kc + 1) * 128, :], 768, Bw)
            load_cast(lambda c0, c1, kc=kc: w_kvb3[:, kc, c0:c1], w_kvb_d[kc * 128:(kc + 1) * 128, :], 1024, Bw)

        if stop == 0:
            finish([])
            return nc
        S_.barrier()
        A.release(m_p1)
        xt = [A.alloc(1024, F32) for _ in range(2)]
        Bxt = [Buf() for _ in range(2)]
        xb = [A.alloc(1024, BF16) for _ in range(2)]
        Bxb = [Buf() for _ in range(2)]
        xT = A.alloc(8 * 512, BF16)
        xT3 = v3(xT, 8)
        BxT = Buf()
        c32 = A.alloc(2 * 512, F32)
        c323 = v3(c32, 2)
        Bc32 = Buf()
        sq = A.alloc(2 * 512, BF16)
        sq3 = v3(sq, 2)
        Bsq = Buf()
        rr = A.alloc(512, F32)
        Brr = Buf()
        kt1 = A.alloc(512, F32)
        kt2 = A.alloc(512, F32)
        Bkt = Buf()
        ks2 = A.alloc(2 * 1536, BF16)
        ks23 = v3(ks2, 2)
        vs = A.alloc(12 * 256, BF16)
        vs4 = vs.rearrange("p (t g d) -> p t g d", t=12, g=2)
        qz = A.alloc(4096, BF16)
        qz5 = qz.rearrange("p (g b s q) -> p g b s q", g=2, b=4, s=4)
        Bks = [Buf() for _ in range(3)]
        Bvs = [Buf() for _ in range(3)]
        Bqs = [Buf()] * 2
        pq_i = A.alloc(768, I32)
        pq_f = A.alloc(768, F32)
        Bpq = Buf()
        Dt = [A.alloc(3 * 128, BF16) for _ in range(2)]
        Dt3 = [v3(d, 3) for d in Dt]
        BDt = [Buf() for _ in range(2)]
        dtmp = A.alloc(128, F32)
        Bdtmp = Buf()
        Psb = [A.alloc(3 * 512, BF16) for _ in range(2)]
        Psb3 = [v3(p, 3) for p in Psb]
        BPsb = [Buf() for _ in range(2)]
        rec = [A.alloc(512, F32) for _ in range(2)]
        Brec = [Buf() for _ in range(2)]

        S_.pool(lambda e: e.memset(vs, 1.0), writes=Bvs)
        S_.pool(lambda e: e.memset(qz, 0.0), writes=[Bqs[0]])
        esrow = A.alloc(1024, BF16)
        zo = A.alloc(128, BF16)
        ntmp = A.alloc(128, F32)
        S_.dve(lambda e: e.memset(zo[0:1, 0:64], 0.0), writes=[B_const])
        S_.dve(lambda e: e.memset(zo[0:1, 64:128], 1.0), writes=[B_const])
        for g in range(2):
            for slot in range(4):
                par, j = slot // 2, slot % 2
                h = 4 * g + 2 * j + par
                S_.dve(lambda e, g=g, slot=slot, h=h: e.tensor_copy(out=esrow[0:1, g * 512 + slot * 128:g * 512 + (slot + 1) * 128],
                                                                    in_=esink[0:1, h:h + 1].to_broadcast([1, 128])),
                       reads=[B_const], writes=[B_const])

        ev_i = [0]

        act_only = [True]

        def evac(out, in_, reads, writes):
            ev_i[0] += 1
            if act_only[0] or ev_i[0] % 2:
                return S_.act(lambda e: e.copy(out=out, in_=in_), reads=reads, writes=writes)
            return S_.dve(lambda e: e.tensor_copy(out=out, in_=in_), reads=reads, writes=writes)

        pbank = [0]

        def nextbank():
            pbank[0] = (pbank[0] + 1) % 3
            return pbank[0]

        def swa_units(c):
            units_ = []
            cs0 = c * 512

            lo = max(0, cs0 - 128)
            hi = min(S, cs0 + 640)

            def load_pq():
                S_.dma("pq", lambda e: e.dma_start(out=pq_i[:, 0:hi - lo], in_=pos_d[0:1, lo:hi].partition_broadcast(128)), writes=[Bpq])
                S_.dve(lambda e: e.tensor_copy(out=pq_f[:, 0:hi - lo], in_=pq_i[:, 0:hi - lo]), reads=[Bpq], writes=[Bpq])

            for qb in range(4):
                i = c * 4 + qb
                offs = [o for o in (-1, 0, 1) if 0 <= i + o < NT]
                no = len(offs)
                di = i % 2
                qcol = (c % 2) * 512 + qb * 128

                def dtiles(i=i, qb=qb, offs=offs, di=di):
                    for n, o in enumerate(offs):
                        kb = i + o
                        pks = pq_f[:, kb * 128 - lo:kb * 128 - lo + 128]
                        if o == 0:
                            S_.act(lambda e, n=n, pks=pks: e.activation(out=Dt3[di][:, n, :], in_=pks, func=AF.Abs, bias=npcol[:, i:i + 1]),
                                   reads=[Bpq, B_const], writes=[BDt[di]])
                        else:
                            mi = 1 if o == -1 else 0
                            S_.act(lambda e, pks=pks: e.activation(out=dtmp, in_=pks, func=AF.Abs, bias=npcol[:, i:i + 1]),
                                   reads=[Bpq, B_const], writes=[Bdtmp])
                            S_.pool(lambda e, n=n, mi=mi: e.tensor_tensor(out=Dt3[di][:, n, :], in0=dtmp, in1=masks3[:, mi, :], op=ALU.add),
                                    reads=[Bdtmp, B_const], writes=[BDt[di]])

                for g in range(2):
                    pi = (i * 2 + g) % 2
                    pvb = (7, 3)[(i * 2 + g) % 2]

                    def A_(i=i, g=g, qb=qb, offs=offs, no=no, di=di, qcol=qcol, pi=pi, first=(qb == 0 and g == 0), dt=(dtiles if g == 0 else None)):
                        if first:
                            load_pq()
                        if dt is not None:
                            dt()
                        for n, o in enumerate(offs):
                            kb = i + o
                            kcol = (kb % 12) * 128
                            kring = (kb // 4) % 3
                            mm(bank(4 + n), ks23[:, g, kcol:kcol + 128], qz5[:, g, qb, :, :].rearrange("p s q -> p (s q)"), True, False,
                               reads=[Bks[kring], Bqs[0]], writes=[PB[4 + n]])
                            mm(bank(4 + n), Dt3[di][:, n, :], dg[:, g * 512:(g + 1) * 512], False, True,
                               reads=[BDt[di], B_const], writes=[PB[4 + n]])
                        S_.act(lambda e: e.activation(out=Psb[pi][:, 0:no * 512], in_=bank(4, no), func=AF.Exp, scale=0.125),
                               reads=[PB[4 + n] for n in range(no)], writes=[BPsb[pi]])

                    def B_(i=i, g=g, offs=offs, no=no, pi=pi, pvb=pvb):
                        for n, o in enumerate(offs):
                            kb = i + o
                            mm(bank(pvb), vs4[:, kb % 12, g, :], Psb3[pi][:, n, :], n == 0, n == no - 1,
                               reads=[Bvs[(kb // 4) % 3], BPsb[pi]], writes=[PB[pvb]])
                            if n == 0:
                                mm(bank(pvb), zo[0:1, :], esrow[0:1, g * 512:(g + 1) * 512], False, False,
                                   reads=[B_const], writes=[PB[pvb]])
                        S_.act(lambda e: e.activation(out=rec[pi][0:64, :], in_=bank(pvb)[64:128, :], func=AF.Ln),
                               reads=[PB[pvb]], writes=[Brec[pi]])
                        S_.act(lambda e: e.activation(out=rec[pi][0:64, :], in_=rec[pi][0:64, :], func=AF.Exp, scale=-1.0),
                               reads=[Brec[pi]], writes=[Brec[pi]])
                        for par in range(2):
                            S_.dve(lambda e, par=par: e.tensor_tensor(
                                out=OTs3[par * 64:par * 64 + 64, 2 * g:2 * g + 2, i * 128:(i + 1) * 128],
                                in0=v3(bank(pvb)[0:64, par * 256:(par + 1) * 256], 2),
                                in1=v3(rec[pi][0:64, par * 256:(par + 1) * 256], 2), op=ALU.mult),
                                reads=[PB[pvb], Brec[pi]])

                    units_.append((A_, B_))
            return units_

        def latent_pieces(c):
            cs = slice(c * 512, (c + 1) * 512)
            pcs = []
            for which, (col0, dst3, gc) in enumerate(((0, cqn3, 0), (256, ckvn3, 2))):
                def lat(m, col0=col0):
                    b = nextbank()
                    for kc in range(8):
                        mm(bank(b), w_in3[:, kc, col0 + m * 128:col0 + (m + 1) * 128], xT3[:, kc, :], kc == 0, kc == 7,
                           reads=[BxT, Bw], writes=[PB[b]])
                    S_.act(lambda e: e.copy(out=c323[:, m, :], in_=bank(b)), reads=[PB[b]], writes=[Bc32])
                    S_.pool(lambda e: e.tensor_tensor(out=sq3[:, m, :], in0=c323[:, m, :], in1=c323[:, m, :], op=ALU.mult),
                            reads=[Bc32], writes=[Bsq])

                def norm(dst3=dst3, gc=gc):
                    b = nextbank()
                    for m in range(2):
                        mm(bank(b), onesb, sq3[:, m, :], m == 0, m == 1, reads=[Bsq, B_const], writes=[PB[b]])
                    S_.act(lambda e: e.activation(out=rr, in_=bank(b), func=AF.Ln, scale=1.0 / 256.0, bias=cst[:, 2:3]),
                           reads=[PB[b], B_const], writes=[Brr])
                    S_.act(lambda e: e.activation(out=rr, in_=rr, func=AF.Exp, scale=-0.5), reads=[Brr], writes=[Brr])
                    for m in range(2):
                        S_.dve(lambda e, m=m: e.scalar_tensor_tensor(
                            out=dst3[:, m, cs], in0=c323[:, m, :], scalar=gcol[:, gc + m:gc + m + 1], in1=rr, op0=ALU.mult, op1=ALU.mult),
                            reads=[Bc32, Brr, B_const])

                pcs.append(lambda lat=lat: lat(0))
                pcs.append(lambda lat=lat: lat(1))
                pcs.append(norm)

            def kr1():
                b1 = nextbank()
                for kc in range(8):
                    mm(bank(b1)[0:96, :], w_in3[:, kc, 448:544], xT3[:, kc, :], kc == 0, kc == 7, reads=[BxT, Bw], writes=[PB[b1]])
                S_.dve(lambda e: e.tensor_tensor(out=kt1[PR, :], in0=bank(b1)[PR, :], in1=ropeC[PR, cs], op=ALU.mult),
                       reads=[PB[b1], Br], writes=[Bkt])

            def kr2():
                b2 = nextbank()
                for kc in range(8):
                    mm(bank(b2)[0:96, :], w_insw3[:, kc, :], xT3[:, kc, :], kc == 0, kc == 7, reads=[BxT, Bw], writes=[PB[b2]])
                S_.dve(lambda e: e.tensor_tensor(out=kt2[PR, :], in0=bank(b2)[PR, :], in1=ropeS[PR, cs], op=ALU.mult),
                       reads=[PB[b2], Br], writes=[Bkt])
                S_.dve(lambda e: e.tensor_tensor(out=krope[PR, cs], in0=kt1[PR, :], in1=kt2[PR, :], op=ALU.add), reads=[Bkt])

            pcs.append(kr1)
            pcs.append(kr2)
            qpcs = []
            for p in range(4):
                def qsp(p=p):
                    b = nextbank()
                    for kc in range(8):
                        mm(bank(b), w_in3[:, kc, 544 + p * 128:544 + (p + 1) * 128], xT3[:, kc, :], kc == 0, kc == 7,
                           reads=[BxT, Bw], writes=[PB[b]])
                    g_, j_ = p // 2, p % 2
                    evac(qz5[0:64, g_, :, j_, :], v3(bank(b)[0:64, :], 4), [PB[b]], [Bqs[0]])
                    evac(qz5[64:128, g_, :, 2 + j_, :], v3(bank(b)[64:128, :], 4), [PB[b]], [Bqs[0]])
                qpcs.append(qsp)
            return pcs, qpcs

        for c in range(NCH):
            cs = slice(c * 512, (c + 1) * 512)
            for t in range(4):
                tt = c * 4 + t
                i = tt % 2
                S_.dma(f"x{i}", lambda e, i=i, tt=tt: e.dma_start(out=xt[i], in_=x_d[tt * 128:(tt + 1) * 128, :]), writes=[Bxt[i]])
                if t % 2 == 0:
                    S_.dve(lambda e, i=i: e.tensor_copy(out=xb[i], in_=xt[i]), reads=[Bxt[i]], writes=[Bxb[i]])
                else:
                    S_.act(lambda e, i=i: e.copy(out=xb[i], in_=xt[i]), reads=[Bxt[i]], writes=[Bxb[i]])
                for half in range(2):
                    b = nextbank()
                    for k4 in range(4):
                        kc = half * 4 + k4
                        S_.pe(lambda e, b=b, k4=k4, kc=kc, i=i: e.transpose(out=bankb(b)[:, k4 * 128:(k4 + 1) * 128],
                                                                             in_=xb[i][:, kc * 128:(kc + 1) * 128], identity=identb),
                              reads=[Bxb[i], B_const], writes=[PB[b]])
                    evac(xT3[:, half * 4:half * 4 + 4, t * 128:(t + 1) * 128], v3(bankb(b)[:, 0:512], 4), [PB[b]], [BxT])
            b = nextbank()
            for kc in range(8):
                mm(bank(b), w_in3[:, kc, 1056:1184], xT3[:, kc, :], kc == 0, kc == 7, reads=[BxT, Bw], writes=[PB[b]])
            kr = c % 3
            kc0 = kr * 512
            for g in range(2):
                for dsth in range(2):
                    evac(ks23[dsth * 64:dsth * 64 + 64, g, kc0:kc0 + 512], bank(b)[g * 64:g * 64 + 64, :], [PB[b]], [Bks[kr]])
            b = nextbank()
            for t in range(4):
                for kc in range(8):
                    mm(bank(b)[:, t * 128:(t + 1) * 128], xT3[:, kc, t * 128:(t + 1) * 128], w_in3[:, kc, 1184:1312], kc == 0, kc == 7,
                       reads=[BxT, Bw], writes=[PB[b]])
            t0 = (c * 4) % 12
            evac(vs4[:, t0:t0 + 4, :, 0:64], bank(b).rearrange("p (t g d) -> p t g d", t=4, g=2), [PB[b]], [Bvs[kr]])
            pcs, qpcs = latent_pieces(c)
            us = swa_units(c - 1) if c >= 1 else []
            if not us:
                for p in pcs:
                    p()
            else:
                per = -(-len(pcs) // len(us))
                for k, (A_, B_) in enumerate(us):
                    A_()
                    for p in pcs[k * per:(k + 1) * per]:
                        p()
                    B_()
                for p in pcs[len(us) * per:]:
                    p()
            for p in qpcs:
                p()
        for (A_, B_) in swa_units(NCH - 1):
            A_()
            B_()

        if stop == 1:
            finish([])
            return nc
        act_only[0] = False
        S_.barrier()
        A.release(m_p12)
        KT = [A.alloc(S, BF16) for _ in range(2)]
        BKT = [Buf() for _ in range(2)]
        Vh = [A.alloc(NT * 128, BF16) for _ in range(2)]
        Vh3 = [v3(v, NT) for v in Vh]
        BVh = [Buf() for _ in range(2)]
        QT = [A.alloc(512, BF16) for _ in range(2)]
        BQT = [Buf() for _ in range(2)]
        PT = [A.alloc(3 * 512, BF16) for _ in range(3)]
        PT3 = [v3(p, 3) for p in PT]
        BPT = [Buf() for _ in range(3)]
        rec2 = [A.alloc(512, F32) for _ in range(2)]
        Brec2 = [Buf() for _ in range(2)]
        qt1 = A.alloc(512, F32)
        qt2 = A.alloc(512, F32)
        Bqt = Buf()
        for i in range(2):
            S_.pool(lambda e, i=i: e.memset(Vh[i], 1.0), writes=[BVh[i]])
        SC = 96.0 ** -0.5
        NSTG = 4
        sg32 = [A.alloc(1024, F32) for _ in range(NSTG)]
        sg16 = [A.alloc(1024, BF16) for _ in range(NSTG)]
        Bsg32 = [Buf() for _ in range(NSTG)]
        Bsg16 = [Buf() for _ in range(NSTG)]
        scr_jobs = []
        for kc in range(8):
            for hf in range(2):
                for j0 in range(0, NJ, 8):
                    scr_jobs.append(("up", kc, hf, j0, min(8, NJ - j0)))
        for j in range(NJ):
            scr_jobs.append(("dn", j, 0, 0, 0))
        scr_state = [0, 0]

        def emit_scratch(n):
            for _ in range(n):
                if scr_state[0] >= len(scr_jobs):
                    return
                kind, a0, hf, j0, nj = scr_jobs[scr_state[0]]
                scr_state[0] += 1
                i = scr_state[1] % NSTG
                scr_state[1] += 1
                if kind == "up":
                    kc = a0
                    w = nj * 128
                    c0 = hf * DFF + j0 * 128
                    S_.dma(f"wl{i}", lambda e, i=i, kc=kc, c0=c0, w=w: e.dma_start(out=sg32[i][:, 0:w], in_=w_up_d[kc * 128:(kc + 1) * 128, c0:c0 + w]),
                           writes=[Bsg32[i]])
                    S_.dve(lambda e, i=i, w=w: e.tensor_copy(out=sg16[i][:, 0:w], in_=sg32[i][:, 0:w]), reads=[Bsg32[i]], writes=[Bsg16[i]])
                    S_.dma(f"ws{i}", lambda e, i=i, kc=kc, hf=hf, j0=j0, nj=nj, w=w: e.dma_start(
                        out=wup_s[j0:j0 + nj, :, kc, hf * 128:(hf + 1) * 128].rearrange("j p f -> p j f"), in_=v3(sg16[i][:, 0:w], nj)),
                        reads=[Bsg16[i]])
                else:
                    j = a0
                    S_.dma(f"wl{i}", lambda e, i=i, j=j: e.dma_start(out=sg32[i], in_=w_dn_d[j * 128:(j + 1) * 128, :]), writes=[Bsg32[i]])
                    S_.dve(lambda e, i=i: e.tensor_copy(out=sg16[i], in_=sg32[i]), reads=[Bsg32[i]], writes=[Bsg16[i]])
                    S_.dma(f"ws{i}", lambda e, i=i, j=j: e.dma_start(out=wdn_s[j * 128:(j + 1) * 128, :], in_=sg16[i]), reads=[Bsg16[i]])

        groups = []
        kb = 0
        while kb < NT:
            n = min(3, NT - kb)
            groups.append((kb, n))
            kb += n
        NG = len(groups)

        def devac(out, in_, reads, writes):
            return S_.dve(lambda e: e.tensor_copy(out=out, in_=in_), reads=reads, writes=writes)

        def head_pieces(h, pb):
            hb = h % 2
            pcs = []

            def kpiece(c):
                cs = slice(c * 512, (c + 1) * 512)
                for kc in range(2):
                    mm(bank(pb)[0:64, :], w_kvb3[:, kc, h * 128:h * 128 + 64], ckvn3[:, kc, cs], kc == 0, kc == 1, writes=[PB[pb]])
                devac(KT[hb][0:64, cs], bank(pb)[0:64, :], [PB[pb]], [BKT[hb]])

            def vpiece(t8):
                for t in range(t8, t8 + 8):
                    for kc in range(2):
                        mm(bank(pb)[:, (t - t8) * 64:(t - t8 + 1) * 64], ckvn3[:, kc, t * 128:(t + 1) * 128],
                           w_kvb3[:, kc, h * 128 + 64:h * 128 + 128], kc == 0, kc == 1, writes=[PB[pb]])
                devac(Vh3[hb][:, t8:t8 + 8, 0:64], v3(bank(pb), 8), [PB[pb]], [BVh[hb]])

            pcs.append(lambda: S_.pool(lambda e: e.tensor_copy(out=KT[hb][PR, :], in_=krope[PR, :]), writes=[BKT[hb]]))
            for c in range(NCH):
                pcs.append(lambda c=c: kpiece(c))
            for t8 in range(0, NT, 8):
                pcs.append(lambda t8=t8: vpiece(t8))
            return pcs

        def q_pieces(h, c, pb):
            cs = slice(c * 512, (c + 1) * 512)
            qi = (h * NCH + c) % 2

            def q1():
                for kc in range(2):
                    mm(bank(pb)[0:96, :], w_qb3[:, kc, h * 96:(h + 1) * 96], cqn3[:, kc, cs], kc == 0, kc == 1, writes=[PB[pb]])
                S_.dve(lambda e: e.tensor_copy(out=QT[qi][0:64, :], in_=bank(pb)[0:64, :]), reads=[PB[pb]], writes=[BQT[qi]])
                S_.dve(lambda e: e.tensor_tensor(out=qt1[PR, :], in0=bank(pb)[PR, :], in1=ropeC[PR, cs], op=ALU.mult),
                       reads=[PB[pb]], writes=[Bqt])

            def q2():
                for kc in range(2):
                    mm(bank(pb)[0:96, :], w_qbsw3[:, kc, h * 96:(h + 1) * 96], cqn3[:, kc, cs], kc == 0, kc == 1, writes=[PB[pb]])
                S_.dve(lambda e: e.tensor_tensor(out=qt2[PR, :], in0=bank(pb)[PR, :], in1=ropeS[PR, cs], op=ALU.mult),
                       reads=[PB[pb]], writes=[Bqt])
                S_.dve(lambda e: e.tensor_tensor(out=QT[qi][PR, :], in0=qt1[PR, :], in1=qt2[PR, :], op=ALU.add),
                       reads=[Bqt], writes=[BQT[qi]])

            return [q1, q2]

        units = [(h, c, g) for h in range(8) for c in range(NCH) for g in range(NG)]

        def u_info(idx):
            h, c, g = units[idx]
            kb0, n = groups[g]
            return h, c, g, kb0, n, (idx % 2) * 3, idx % 3, (h * NCH + c) % 2, h % 2

        def emit_qk(idx):
            h, c, g, kb0, n, sb0, pi, qi, hb = u_info(idx)
            for j in range(n):
                kbj = kb0 + j
                mm(bank(sb0 + j), KT[hb][0:96, kbj * 128:(kbj + 1) * 128], QT[qi][0:96, :], True, True,
                   reads=[BKT[hb], BQT[qi]], writes=[PB[sb0 + j]])

        def emit_exp(idx):
            h, c, g, kb0, n, sb0, pi, qi, hb = u_info(idx)
            S_.act(lambda e: e.activation(out=PT[pi][:, 0:n * 512], in_=bank(sb0, n), func=AF.Exp, scale=SC),
                   reads=[PB[sb0 + j] for j in range(n)], writes=[BPT[pi]])

        def emit_pv(idx):
            h, c, g, kb0, n, sb0, pi, qi, hb = u_info(idx)
            ab = 6 + (h * NCH + c) % 2
            for j in range(n):
                kbj = kb0 + j
                mm(bank(ab), Vh3[hb][:, kbj, :], PT3[pi][:, j, :], kbj == 0, kbj == NT - 1,
                   reads=[BVh[hb], BPT[pi]], writes=[PB[ab]])

        def emit_norm(h, c):
            cs = slice(c * 512, (c + 1) * 512)
            m = h * NCH + c
            ab = 6 + m % 2
            ri = m % 2
            par = h % 2
            S_.dve(lambda e: e.reciprocal(out=rec2[ri][0:64, :], in_=bank(ab)[64:128, :]), reads=[PB[ab]], writes=[Brec2[ri]])
            S_.dve(lambda e: e.tensor_tensor(out=OTm3[par * 64:par * 64 + 64, h // 2, cs], in0=bank(ab)[0:64, :],
                                             in1=rec2[ri][0:64, :], op=ALU.mult), reads=[PB[ab], Brec2[ri]])

        hp6, hp7 = head_pieces(0, 6), head_pieces(0, 7)
        for k in range(len(hp6)):
            (hp6 if k % 2 == 0 else hp7)[k]()
        for p in q_pieces(0, 0, 6):
            p()
        emit_qk(0)
        emit_qk(1)
        sched = {}
        for m in range(8 * NCH):
            h, c = m // NCH, m % NCH
            u0 = m * NG
            ob = 6 + (m + 1) % 2
            pcs = []
            if c == NCH - 1 and h + 1 < 8:
                pcs += head_pieces(h + 1, ob)
            if m + 1 < 8 * NCH:
                q1, q2 = q_pieces((m + 1) // NCH, (m + 1) % NCH, ob)
                qslot = (max(0, min(NG - 3, 3)), max(0, min(NG - 3, 6)))
            else:
                q1 = q2 = None
            g0 = min(2, max(0, NG - 3))
            L = max(1, NG - 2 - g0)
            per = -(-len(pcs) // L) if pcs else 0
            for g in range(NG):
                lst = []
                if q1 is not None and g == qslot[0]:
                    lst.append(q1)
                k = g - g0
                if per and 0 <= k < L:
                    lst += pcs[k * per:(k + 1) * per] if k < L - 1 else pcs[k * per:]
                if q2 is not None and g == qslot[1]:
                    lst.append(q2)
                sched[u0 + g] = lst
        for idx in range(len(units)):
            h, c, g = units[idx]
            for p in sched.get(idx, []):
                p()
            if g == 5 and not (h == 0 and c == 0):
                emit_scratch(2)
            emit_exp(idx)
            if idx + 2 < len(units):
                emit_qk(idx + 2)
            emit_pv(idx)
            if g == NG - 1:
                emit_norm(h, c)
        emit_scratch(10 ** 6)

        if stop == 2:
            finish([])
            return nc
        S_.barrier()
        A.release(m_p3)
        lnp = A.alloc(4 * D, F32)
        lnp3 = v3(lnp, 4)
        w_o = A.alloc(8 * D, BF16)
        w_o3 = v3(w_o, 8)
        xres = [A.alloc(D, F32) for _ in range(2)]
        Bxres = [Buf() for _ in range(2)]
        NX1 = 9
        x1 = [A.alloc(D, F32) for _ in range(NX1)]
        Bx1 = [Buf() for _ in range(NX1)]
        x1T = A.alloc(8 * 1024, BF16)
        x1T3 = v3(x1T, 8)
        Bx1T = [Buf() for _ in range(8)]
        AT = A.alloc(NJ * 512, BF16)
        AT3 = v3(AT, NJ)
        BAT = [Buf() for _ in range(NJ)]
        wub = [A.alloc(8 * 256, BF16) for _ in range(2)]
        wub3 = [v3(w, 8) for w in wub]
        Bwub = [Buf() for _ in range(2)]
        NWD = 4
        wdb = [A.alloc(D, BF16) for _ in range(NWD)]
        Bwdb = [Buf() for _ in range(NWD)]
        hacc = [A.alloc(512, F32) for _ in range(4)]
        Bhacc = [Buf() for _ in range(4)]
        stats = [A.alloc(32, F32) for _ in range(4)]
        Bstats = [Buf() for _ in range(4)]
        ln_i = [0]
        hbuf = A.alloc(16, BF16)
        hbuf3 = v3(hbuf, 8)
        Bhb = Buf()
        Blnp = Buf()
        for k in range(4):
            S_.dma("c3", lambda e, k=k: e.dma_start(out=lnp3[:, k, :], in_=ln_d[k:k + 1, :].partition_broadcast(128)), writes=[Blnp])
        Bwo = Buf()
        stg3 = [hacc[0], hacc[1]]
        Bstg3 = [Bhacc[0], Bhacc[1]]
        for kc in range(8):
            load_cast(lambda c0, c1, kc=kc: w_o3[:, kc, c0:c1], w_o_d[kc * 128:(kc + 1) * 128, :], D, Bwo, stg3, Bstg3)

        def layer_norm(buf, Bb, gk):
            k = ln_i[0] % 4
            ln_i[0] += 1
            stat, Bstat = stats[k], Bstats[k]
            for hh in range(2):
                S_.dve(lambda e, hh=hh: e.bn_stats(out=stat[:, hh * 6:(hh + 1) * 6], in_=buf[:, hh * 512:(hh + 1) * 512]),
                       reads=[Bb], writes=[Bstat])
            S_.dve(lambda e: e.bn_aggr(out=stat[:, 16:18], in_=stat[:, 0:12]), reads=[Bstat], writes=[Bstat])
            S_.act(lambda e: e.activation(out=stat[:, 18:19], in_=stat[:, 17:18], func=AF.Sqrt, bias=cst[:, 3:4]),
                   reads=[Bstat, B_const], writes=[Bstat])
            S_.dve(lambda e: e.reciprocal(out=stat[:, 19:20], in_=stat[:, 18:19]), reads=[Bstat], writes=[Bstat])
            S_.dve(lambda e: e.scalar_tensor_tensor(out=buf, in0=buf, scalar=stat[:, 16:17], in1=lnp3[:, gk, :], op0=ALU.subtract, op1=ALU.mult),
                   reads=[Bb, Bstat, Blnp], writes=[Bb])
            return S_.dve(lambda e: e.scalar_tensor_tensor(out=buf, in0=buf, scalar=stat[:, 19:20], in1=lnp3[:, gk + 1, :], op0=ALU.mult, op1=ALU.add),
                          reads=[Bb, Bstat, Blnp], writes=[Bb])

        def post_a1(t, b0):
            xi = t % 2
            s = t % NX1
            S_.dma(f"x{xi}", lambda e: e.dma_start(out=xres[xi], in_=x_d[t * 128:(t + 1) * 128, :]), writes=[Bxres[xi]])
            for nh in range(2):
                for kp in range(8):
                    src = OTm3[:, kp, t * 128:(t + 1) * 128] if kp < 4 else OTs3[:, kp - 4, t * 128:(t + 1) * 128]
                    mm(bank(b0 + nh), src, w_o3[:, kp, nh * 512:(nh + 1) * 512], kp == 0, kp == 7, reads=[Bwo], writes=[PB[b0 + nh]])
                S_.dve(lambda e, nh=nh: e.scalar_tensor_tensor(out=x1[s][:, nh * 512:(nh + 1) * 512], in0=xres[xi][:, nh * 512:(nh + 1) * 512],
                                                               scalar=ALPHA, in1=bank(b0 + nh), op0=ALU.mult, op1=ALU.add),
                       reads=[Bxres[xi], PB[b0 + nh]], writes=[Bx1[s]])

        def post_a1_ln(t):
            s = t % NX1
            layer_norm(x1[s], Bx1[s], 0)

        def post_a2(t, b0):
            s = t % NX1
            rc = (t % 8) * 128
            for half in range(2):
                b = b0 + half
                for k4 in range(4):
                    kc = half * 4 + k4
                    S_.pe(lambda e, b=b, k4=k4, kc=kc: e.transpose(out=bank(b)[:, k4 * 128:(k4 + 1) * 128], in_=x1[s][:, kc * 128:(kc + 1) * 128],
                                                                   identity=ident32), reads=[Bx1[s], B_const], writes=[PB[b]])
                evac(x1T3[:, half * 4:half * 4 + 4, rc:rc + 128], v3(bank(b), 4), [PB[b]], [Bx1T[t % 8]])

        out_ops = []
        deferred = []
        deferred_ln = []

        def ffn(c, mid=None, post=None):
            base = (c % 2) * 512
            right = (base + 512) % 1024
            rd = [Bx1T[(4 * c + k) % 8] for k in range(4)]
            rdh = [Bx1T[(4 * c - 1) % 8], Bx1T[(4 * c + 4) % 8]]
            has_l, has_r = c > 0, c < NCH - 1

            def load_dn(j):
                i = (c * NJ + j) % NWD
                S_.dma(f"wd{i}", lambda e: e.dma_start(out=wdb[i], in_=wdn_s[j * 128:(j + 1) * 128, :]), writes=[Bwdb[i]])

            def load_up(j):
                i = (c * NJ + j) % 2
                S_.dma(f"wu{i}", lambda e: e.dma_start(out=wub3[i], in_=wup_s[j]), writes=[Bwub[i]])

            load_up(0)
            left = (base - 1) % 1024
            S_.pool(lambda e: e.tensor_copy(out=hbuf3[:, :, 0:1], in_=x1T3[:, :, right:right + 1]), reads=rdh, writes=[Bhb])
            S_.pool(lambda e: e.tensor_copy(out=hbuf3[:, :, 1:2], in_=x1T3[:, :, left:left + 1]), reads=rdh, writes=[Bhb])
            for j in range(NJ):
                if j + 1 < NJ:
                    load_up(j + 1)
                if j == NJ - 4:
                    for jj in range(NWD - 1):
                        load_dn(jj)
                if j in (2, 5, 8, 11) and deferred_ln:
                    deferred_ln.pop(0)()
                if j == 14:
                    while deferred:
                        deferred.pop(0)()
                wi = (c * NJ + j) % 2
                accs = []
                for half in range(2):
                    f = half * NJ + j
                    hb_ = (j * 2 + half) % 6
                    xb_ = 6 + half
                    ai = (j * 2 + half) % 4
                    for kc in range(8):
                        mm(bank(hb_), wub3[wi][:, kc, half * 128:(half + 1) * 128], x1T3[:, kc, base:base + 512], kc == 0, kc == 7,
                           reads=[Bwub[wi]] + rd, writes=[PB[hb_]])
                    if has_l or has_r:
                        for kc in range(8):
                            mm(bank(xb_)[:, 0:2], wub3[wi][:, kc, half * 128:(half + 1) * 128],
                               hbuf3[:, kc, :], kc == 0, kc == 7, reads=[Bwub[wi], Bhb], writes=[PB[xb_]])
                    w0 = convp[:, 0 * 44 + f:0 * 44 + f + 1]
                    w1 = convp[:, 1 * 44 + f:1 * 44 + f + 1]
                    w2 = convp[:, 2 * 44 + f:2 * 44 + f + 1]
                    bb = convp[:, 3 * 44 + f:3 * 44 + f + 1]
                    acc = hacc[ai]
                    S_.act(lambda e, acc=acc, hb_=hb_, w1=w1, bb=bb: e.activation(out=acc, in_=bank(hb_), func=AF.Identity, scale=w1, bias=bb),
                           reads=[PB[hb_], B_const], writes=[Bhacc[ai]])
                    S_.dve(lambda e, acc=acc, hb_=hb_, w0=w0: e.scalar_tensor_tensor(out=acc[:, 1:512], in0=bank(hb_)[:, 0:511], scalar=w0,
                                                                                     in1=acc[:, 1:512], op0=ALU.mult, op1=ALU.add),
                           reads=[PB[hb_], Bhacc[ai], B_const], writes=[Bhacc[ai]])
                    S_.dve(lambda e, acc=acc, hb_=hb_, w2=w2: e.scalar_tensor_tensor(out=acc[:, 0:511], in0=bank(hb_)[:, 1:512], scalar=w2,
                                                                                     in1=acc[:, 0:511], op0=ALU.mult, op1=ALU.add),
                           reads=[PB[hb_], Bhacc[ai], B_const], writes=[Bhacc[ai]])
                    if has_l:
                        S_.dve(lambda e, acc=acc, xb_=xb_, w0=w0: e.scalar_tensor_tensor(out=acc[:, 0:1], in0=bank(xb_)[:, 1:2], scalar=w0,
                                                                                         in1=acc[:, 0:1], op0=ALU.mult, op1=ALU.add),
                               reads=[PB[xb_], Bhacc[ai], B_const], writes=[Bhacc[ai]])
                    if has_r:
                        S_.dve(lambda e, acc=acc, xb_=xb_, w2=w2: e.scalar_tensor_tensor(out=acc[:, 511:512], in0=bank(xb_)[:, 0:1], scalar=w2,
                                                                                         in1=acc[:, 511:512], op0=ALU.mult, op1=ALU.add),
                               reads=[PB[xb_], Bhacc[ai], B_const], writes=[Bhacc[ai]])
                    accs.append((acc, Bhacc[ai]))
                (ag, Bg), (au, Bu) = accs
                S_.act(lambda e, ag=ag: e.activation(out=ag, in_=ag, func=AF.Gelu), reads=[Bg], writes=[Bg])
                S_.pool(lambda e, ag=ag, au=au, j=j: e.tensor_tensor(out=AT3[:, j, :], in0=ag, in1=au, op=ALU.mult),
                        reads=[Bg, Bu], writes=[BAT[j]])
            if mid is not None:
                mid()
            for j in range(NJ):
                if j + NWD - 1 < NJ:
                    load_dn(j + NWD - 1)
                wi = (c * NJ + j) % NWD
                for tl in range(4):
                    for nh in range(2):
                        b = tl * 2 + nh
                        mm(bank(b), AT3[:, j, tl * 128:(tl + 1) * 128], wdb[wi][:, nh * 512:(nh + 1) * 512], j == 0, j == NJ - 1,
                           reads=[BAT[j], Bwdb[wi]], writes=[PB[b]])
            for tl in range(4):
                t = c * 4 + tl
                s = t % NX1
                for nh in range(2):
                    b = tl * 2 + nh
                    S_.dve(lambda e, s=s, nh=nh, b=b: e.scalar_tensor_tensor(out=x1[s][:, nh * 512:(nh + 1) * 512],
                                                                             in0=x1[s][:, nh * 512:(nh + 1) * 512], scalar=ALPHA,
                                                                             in1=bank(b), op0=ALU.mult, op1=ALU.add),
                           reads=[PB[b], Bx1[s]], writes=[Bx1[s]])
            if post is not None:
                post()
            for tl in range(4):
                t = c * 4 + tl
                s = t % NX1
                deferred_ln.append(lambda s=s: layer_norm(x1[s], Bx1[s], 2))
                deferred.append(lambda s=s, t=t: out_ops.append(
                    S_.dma(f"o{t % 4}", lambda e: e.dma_start(out=out_d[t * 128:(t + 1) * 128, :], in_=x1[s]), reads=[Bx1[s]])))

        for k, t in enumerate(range(min(NT, 5))):
            post_a1(t, (k % 4) * 2)
        for k, t in enumerate(range(min(NT, 5))):
            post_a1_ln(t)
        for k, t in enumerate(range(min(NT, 5))):
            post_a2(t, (k % 4) * 2)
        for c in range(NCH):
            nxt = list(range(4 * c + 5, min(NT, 4 * c + 9)))

            def mid(nxt=nxt):
                for k, t in enumerate(nxt):
                    post_a1(t, k * 2)
                for k, t in enumerate(nxt):
                    post_a1_ln(t)

            def post(nxt=nxt):
                for k, t in enumerate(nxt):
                    post_a2(t, k * 2)

            ffn(c, mid, post)

        while deferred_ln:
            deferred_ln.pop(0)()
        while deferred:
            deferred.pop(0)()
        finish(out_ops[-4:])
    return nc


def _consts():
    inv_freq = (10000.0 ** (-np.arange(0, 32, 2, dtype=np.float32) / 32.0)).astype(np.float32)
    cst = np.zeros((128, 8), np.float32)
    for p in range(64, 96):
        j = (p - 64) % 16
        cst[p, 0] = inv_freq[j] / (2.0 * math.pi)
        cst[p, 1] = (-1.0 if p < 80 else 1.0) * 2.0 * math.pi
    cst[:, 2] = RMS_EPS
    cst[:, 3] = LN_EPS
    slopes = (2.0 ** (-8.0 * (np.arange(8, dtype=np.float32) + 1.0) / 8.0)).astype(np.float32)
    slp = np.broadcast_to((-8.0 * slopes)[None, :], (128, 8)).astype(np.float32).copy()
    k = np.arange(128)[:, None]
    q = np.arange(128)[None, :]
    masks = np.zeros((2, 128, 128), np.float32)
    masks[0] = np.where(k >= q, 0.0, BIGM)
    masks[1] = np.where(k <= q, 0.0, BIGM)
    return cst, slp, masks, np.eye(128, dtype=np.float32)


def make_in_maps(S, x, positions, w_in, q_norm_g, w_q_b, kv_norm_g, w_kv_b, swa_sinks, w_o, ln1_g, ln1_b, w_up, conv_w,
                 conv_b, w_down, ln2_g, ln2_b):
    f = lambda a: np.ascontiguousarray(np.asarray(a, dtype=np.float32))
    cst, slp, masks, ident = _consts()
    perm = np.concatenate([np.arange(16, 32), np.arange(0, 16)])
    w_in = f(w_in)
    w_insw = np.concatenate([w_in[:, 448:512], w_in[:, 512 + perm]], axis=1)
    w_q_b = f(w_q_b)
    w_qbsw = w_q_b.copy()
    for h in range(8):
        w_qbsw[:, h * 96 + 64:h * 96 + 96] = w_q_b[:, h * 96 + 64 + perm]
    graw = np.stack([f(q_norm_g)[0:128], f(q_norm_g)[128:256], f(kv_norm_g)[0:128], f(kv_norm_g)[128:256]])
    lnp = np.stack([f(ln1_g), f(ln1_b), f(ln2_g), f(ln2_b)])
    cw = f(conv_w).reshape(3, 2 * DFF)
    convp = np.stack([cw[0].reshape(44, 128), cw[1].reshape(44, 128), cw[2].reshape(44, 128), f(conv_b).reshape(44, 128)])
    shared = {
        "w_in": w_in, "w_insw": f(w_insw), "w_qb": w_q_b, "w_qbsw": f(w_qbsw), "w_kvb": f(w_kv_b), "w_o": f(w_o),
        "w_up": f(w_up), "w_down": f(w_down), "graw": f(graw), "sinks": f(swa_sinks).reshape(1, 8), "lnp": f(lnp),
        "convp": f(convp), "ident": ident, "cst": cst, "masks": masks, "slopes": slp,
    }
    x = np.asarray(x, dtype=np.float32)
    positions = np.asarray(positions, dtype=np.int32)
    maps = []
    for b in range(x.shape[0]):
        m = dict(shared)
        m["x"] = np.ascontiguousarray(x[b])
        m["pos"] = np.ascontiguousarray(positions[b].reshape(1, S))
        m["posr"] = np.ascontiguousarray(positions[b].reshape(S // 128, 128))
        maps.append(m)
    return maps


_NC_CACHE = {}


def kernel(**inputs):
    x = np.asarray(inputs["x"])
    B, S, _ = x.shape
    if S not in _NC_CACHE:
        _NC_CACHE[S] = build(S)
    nc = _NC_CACHE[S]
    maps = make_in_maps(S, **inputs)
    res = run_bass_kernel_spmd(nc, maps, core_ids=list(range(B)))
    return np.stack([np.asarray(r["out"], dtype=np.float32) for r in res.results], axis=0)
```

```python
import contextlib
import math
import numpy as np
import concourse.bass as bass
import concourse.mybir as mybir
from concourse.bass_utils import run_bass_kernel_spmd

F32 = mybir.dt.float32
BF16 = mybir.dt.bfloat16
I32 = mybir.dt.int32
AF = mybir.ActivationFunctionType
ALU = mybir.AluOpType

D = 1024
DFF = 2816
NJ = 22
IN_W = 1312
ALPHA = 2.0 ** 0.25
LN_EPS = 1e-5
RMS_EPS = 1e-6
BIGM = 32768.0
ENGS = ("pe", "act", "dve", "pool", "sp")


class Op:
    __slots__ = ("eng", "fn", "deps", "needed", "seq", "key", "is_dma", "idx")

    def __init__(self, eng, fn, key, is_dma):
        self.eng = eng
        self.fn = fn
        self.deps = {}
        self.needed = False
        self.seq = 0
        self.key = key
        self.is_dma = is_dma
        self.idx = 0


class Buf:
    __slots__ = ("name", "writers", "readers")

    def __init__(self, name=""):
        self.name = name
        self.writers = {}
        self.readers = {}


class Sched:
    def __init__(self, nc):
        self.nc = nc
        self.ops = {e: [] for e in ENGS}
        self.stream_cnt = {}
        self.last = {}
        self.bar = {}

    def _add_dep(self, op, d):
        if d is None or d is op:
            return
        if d.key == op.key and op.eng == "pe" and not op.is_dma:
            return
        cur = op.deps.get(d.key)
        if cur is None or d.idx > cur.idx:
            op.deps[d.key] = d

    def barrier(self):
        self.bar = dict(self.last)

    def op(self, eng, fn, reads=(), writes=(), deps=(), stream=None):
        is_dma = stream is not None
        key = stream if is_dma else eng
        o = Op(eng, fn, key, is_dma)
        if is_dma:
            self.stream_cnt[stream] = self.stream_cnt.get(stream, 0) + 1
            o.idx = self.stream_cnt[stream]
        else:
            o.idx = len(self.ops[eng]) + 1
        for d in self.bar.values():
            self._add_dep(o, d)
        for d in deps:
            self._add_dep(o, d)
        for b in reads:
            for w in b.writers.values():
                self._add_dep(o, w)
        for b in writes:
            for r in b.readers.values():
                self._add_dep(o, r)
            for w in b.writers.values():
                self._add_dep(o, w)
        for b in reads:
            b.readers[key] = o
        for b in writes:
            if b.readers:
                b.readers = {}
                b.writers = {}
            b.writers[key] = o
        for d in o.deps.values():
            d.needed = True
        self.ops[eng].append(o)
        self.last[key] = o
        return o

    def pe(self, fn, **kw):
        return self.op("pe", fn, **kw)

    def act(self, fn, **kw):
        return self.op("act", fn, **kw)

    def dve(self, fn, **kw):
        return self.op("dve", fn, **kw)

    def pool(self, fn, **kw):
        return self.op("pool", fn, **kw)

    def dma(self, stream, fn, eng="sp", **kw):
        return self.op(eng, fn, stream=stream, **kw)

    def emit(self, final_waits=()):
        nc = self.nc
        for e in ENGS:
            n = 0
            for o in self.ops[e]:
                if not o.is_dma and o.needed:
                    n += 1
                    o.seq = n
        with contextlib.ExitStack() as st:
            sems = {}
            for e in ENGS:
                sems[e] = st.enter_context(nc.semaphore("s_" + e))
            for s in self.stream_cnt:
                sems[s] = st.enter_context(nc.semaphore("d_" + s))
            block = st.enter_context(nc.Block())

            def run(e, engobj):
                waited = {}

                def wait(d):
                    val = 16 * d.idx if d.is_dma else d.seq
                    if waited.get(d.key, 0) < val:
                        engobj.wait_ge(sems[d.key], val)
                        waited[d.key] = val

                for o in self.ops[e]:
                    for d in o.deps.values():
                        wait(d)
                    ins = o.fn(engobj)
                    if o.is_dma:
                        ins.then_inc(sems[o.key], 16)
                    elif o.needed:
                        ins.then_inc(sems[o.key], 1)
                if e == "sp":
                    for d in final_waits:
                        wait(d)

            @block.tensor
            def _(e):
                run("pe", e)

            @block.scalar
            def _(e):
                run("act", e)

            @block.vector
            def _(e):
                run("dve", e)

            @block.gpsimd
            def _(e):
                run("pool", e)

            @block.sync
            def _(e):
                run("sp", e)


class _Done(Exception):
    pass


class Arena:
    def __init__(self, ap, n):
        self.ap = ap
        self.n = n
        self.off = 0

    def alloc(self, cols, dt):
        units = cols * (2 if dt in (F32, I32) else 1)
        units = (units + 15) // 16 * 16
        a = self.ap[:, self.off:self.off + units]
        self.off += units
        assert self.off <= self.n, f"arena overflow {self.off} > {self.n}"
        if dt != BF16:
            a = a.bitcast(dt)
        return a[:, 0:cols]

    def mark(self):
        return self.off

    def release(self, m):
        self.off = m


def v3(ap, a):
    return ap.rearrange("p (a b) -> p a b", a=a)


def build(S, dbg=None, stop=None):
    NCH = S // 512
    NT = S // 128
    nc = bass.Bass("TRN2", target_bir_lowering=False)

    def din(name, shape, dt):
        return nc.dram_tensor(name, shape, dt, kind="ExternalInput").ap()

    x_d = din("x", [S, D], F32)
    pos_d = din("pos", [1, S], I32)
    posr_d = din("posr", [NT, 128], I32)
    w_in_d = din("w_in", [D, IN_W], F32)
    w_insw_d = din("w_insw", [D, 96], F32)
    w_qb_d = din("w_qb", [256, 768], F32)
    w_qbsw_d = din("w_qbsw", [256, 768], F32)
    w_kvb_d = din("w_kvb", [256, 1024], F32)
    w_o_d = din("w_o", [D, D], F32)
    w_up_d = din("w_up", [D, 2 * DFF], F32)
    w_dn_d = din("w_down", [DFF, D], F32)
    graw_d = din("graw", [4, 128], F32)
    sinks_d = din("sinks", [1, 8], F32)
    ln_d = din("lnp", [4, D], F32)
    convp_d = din("convp", [4, 44, 128], F32)
    ident_d = din("ident", [128, 128], F32)
    cst_d = din("cst", [128, 8], F32)
    masks_d = din("masks", [2, 128, 128], F32)
    slopes_d = din("slopes", [128, 8], F32)
    out_d = nc.dram_tensor("out", [S, D], F32, kind="ExternalOutput").ap()
    wup_s = nc.dram_tensor("wup_s", [NJ, 128, 8, 256], BF16).ap()
    wdn_s = nc.dram_tensor("wdn_s", [DFF, D], BF16).ap()
    dbg_d = None
    if dbg is not None:
        dbg_d = nc.dram_tensor("dbg", [128, dbg[1]], dbg[2], kind="ExternalOutput").ap()

    st = contextlib.ExitStack()
    with st:
        ARENA_N = 106000
        arena_t = st.enter_context(nc.sbuf_tensor("arena", [128, ARENA_N], BF16))
        PS = st.enter_context(nc.psum_tensor("ps", [128, 4096], F32))
        A = Arena(arena_t, ARENA_N)
        S_ = Sched(nc)
        PB = [Buf(f"bank{i}") for i in range(8)]

        def bank(i, n=1):
            return PS[:, i * 512:(i + n) * 512]

        def bankb(i):
            return PS[:, i * 512:(i + 1) * 512].bitcast(BF16)

        def mm(out, lhsT, rhs, start, stop, **kw):
            return S_.pe(lambda e: e.matmul(out, lhsT=lhsT, rhs=rhs, start=start, stop=stop), **kw)

        identb = A.alloc(128, BF16)
        onesb = A.alloc(128, BF16)
        ident32 = A.alloc(128, F32)
        cst = A.alloc(8, F32)
        slp = A.alloc(8, F32)
        gcol = A.alloc(4, F32)
        esink = A.alloc(8, F32)
        convp = A.alloc(4 * 44, F32)
        OTs = A.alloc(4 * S, BF16)
        OTs3 = v3(OTs, 4)
        m_all = A.mark()
        OTM_N = max(4 * S, 16384)
        OTm_full = A.alloc(OTM_N, BF16)
        OTm = OTm_full[:, 0:4 * S]
        OTm3 = v3(OTm, 4)
        m_p3 = A.mark()
        A.release(m_all)
        A.off = m_p3
        cqn = A.alloc(2 * S, BF16)
        ckvn = A.alloc(2 * S, BF16)
        cqn3, ckvn3 = v3(cqn, 2), v3(ckvn, 2)
        krope = A.alloc(S, BF16)
        ropeC = A.alloc(S, BF16)
        ropeS = A.alloc(S, BF16)
        w_qb = A.alloc(2 * 768, BF16)
        w_qbsw = A.alloc(2 * 768, BF16)
        w_kvb = A.alloc(2 * 1024, BF16)
        w_qb3, w_qbsw3, w_kvb3 = v3(w_qb, 2), v3(w_qbsw, 2), v3(w_kvb, 2)
        m_p12 = A.mark()

        B_const = Buf("const")
        dbg_srcs = dict(OTs=OTs, OTm=OTm, cqn=cqn, ckvn=ckvn, krope=krope, ropeC=ropeC, ropeS=ropeS)

        def finish(final_ops):
            finals = list(final_ops)
            if dbg is not None:
                S_.barrier()
                src = dbg_srcs[dbg[0]]
                finals.append(S_.dma("dbg", lambda e: e.dma_start(out=dbg_d, in_=src)))
            S_.emit(final_waits=finals)

        A2 = Arena(OTm_full, OTM_N)
        stg32 = [A2.alloc(512, F32) for _ in range(2)]
        stg16 = [A2.alloc(512, BF16) for _ in range(2)]
        Bstg32 = [Buf() for _ in range(2)]
        Bstg16 = [Buf() for _ in range(2)]
        masks = A2.alloc(2 * 128, F32)
        masks3 = v3(masks, 2)
        dg = A2.alloc(8 * 128, BF16)
        dg3 = v3(dg, 8)
        pcol = A2.alloc(NT, F32)
        npcol = A2.alloc(NT, F32)
        w_in = A2.alloc(8 * IN_W, BF16)
        w_in3 = v3(w_in, 8)
        w_insw = A2.alloc(8 * 96, BF16)
        w_insw3 = v3(w_insw, 8)
        m_p1 = A.mark()

        S_.dma("k1", lambda e: e.dma_start(out=ident32, in_=ident_d), writes=[B_const])
        S_.dma("k2", lambda e: e.dma_start(out=cst, in_=cst_d), writes=[B_const])
        S_.dma("k3", lambda e: e.dma_start(out=slp, in_=slopes_d), writes=[B_const])
        S_.dma("k4", lambda e: e.dma_start(out=masks3, in_=masks_d.rearrange("m p f -> p m f")), writes=[B_const])
        S_.dma("k5", lambda e: e.dma_start(out=esink, in_=sinks_d.partition_broadcast(128)), writes=[B_const])
        S_.dve(lambda e: e.tensor_copy(out=identb, in_=ident32), reads=[B_const], writes=[B_const])
        S_.dve(lambda e: e.memset(onesb, 1.0), writes=[B_const])
        for g in range(2):
            for slot in range(4):
                par, j = slot // 2, slot % 2
                h = 4 * g + 2 * j + par
                S_.dve(lambda e, h=h, k=g * 4 + slot: e.tensor_scalar(out=dg3[:, k, :], in0=ident32, scalar1=slp[:, h:h + 1], scalar2=None,
                                                                      op0=ALU.mult), reads=[B_const], writes=[B_const])
        S_.act(lambda e: e.activation(out=esink, in_=esink, func=AF.Exp), reads=[B_const], writes=[B_const])

        if stop == 0.1:
            finish([])
            return nc
        tmpA = A.alloc(4 * 128, F32)
        tmpA3 = v3(tmpA, 4)
        tmpI = A.alloc(128, I32)
        tmpF = A.alloc(128, F32)
        Btmp = Buf()
        S_.dma("k6", lambda e: e.dma_start(out=tmpA3[0:44, :, :], in_=convp_d.rearrange("k c p -> c k p")), writes=[Btmp])
        for k in range(4):
            S_.pe(lambda e, k=k: e.transpose(out=bank(0)[:, k * 44:(k + 1) * 44], in_=tmpA3[0:44, k, :], identity=ident32[0:44, 0:44]),
                  reads=[Btmp, B_const], writes=[PB[0]])
        S_.dve(lambda e: e.tensor_copy(out=convp, in_=bank(0)[:, 0:176]), reads=[PB[0]], writes=[B_const])
        tmpG = A.alloc(128, F32)
        BtmpG = Buf()
        S_.dma("k7", lambda e: e.dma_start(out=tmpG[0:4, :], in_=graw_d), writes=[BtmpG])
        S_.pe(lambda e: e.transpose(out=bank(1)[:, 0:4], in_=tmpG[0:4, :], identity=ident32[0:4, 0:4]),
              reads=[BtmpG, B_const], writes=[PB[1]])
        S_.dve(lambda e: e.tensor_copy(out=gcol, in_=bank(1)[:, 0:4]), reads=[PB[1]], writes=[B_const])
        BtmpP = Buf()
        S_.dma("k8", lambda e: e.dma_start(out=tmpI[0:NT, :], in_=posr_d), writes=[BtmpP])
        S_.dve(lambda e: e.tensor_copy(out=tmpF[0:NT, :], in_=tmpI[0:NT, :]), reads=[BtmpP], writes=[BtmpP])
        S_.pe(lambda e: e.transpose(out=bank(2)[:, 0:NT], in_=tmpF[0:NT, :], identity=ident32[0:NT, 0:NT]),
              reads=[BtmpP, B_const], writes=[PB[2]])
        S_.dve(lambda e: e.tensor_copy(out=pcol, in_=bank(2)[:, 0:NT]), reads=[PB[2]], writes=[B_const])
        S_.dve(lambda e: e.tensor_scalar(out=npcol, in0=bank(2)[:, 0:NT], scalar1=-1.0, scalar2=None, op0=ALU.mult), reads=[PB[2]], writes=[B_const])

        if stop == 0.2:
            finish([])
            return nc
        r_i = A.alloc(512, I32)
        r_f = A.alloc(512, F32)
        r_t = A.alloc(512, F32)
        r_u = A.alloc(512, F32)
        r_k = A.alloc(512, I32)
        Br = Buf()
        PR = slice(64, 96)
        for c in range(NCH):
            cs = slice(c * 512, (c + 1) * 512)
            S_.dma("rp", lambda e, cs=cs: e.dma_start(out=r_i[PR, :], in_=pos_d[0:1, cs].partition_broadcast(32)), writes=[Br], eng="pool")
            S_.dve(lambda e: e.tensor_copy(out=r_f[PR, :], in_=r_i[PR, :]), reads=[Br], writes=[Br])
            for which in range(2):
                S_.dve(lambda e, which=which: e.tensor_scalar(out=r_t[PR, :], in0=r_f[PR, :], scalar1=cst[PR, 0:1],
                                                              scalar2=(0.25 if which == 0 else 0.0), op0=ALU.mult, op1=ALU.add),
                       reads=[Br, B_const], writes=[Br])
                S_.dve(lambda e: e.tensor_copy(out=r_k[PR, :], in_=r_t[PR, :]), reads=[Br], writes=[Br])
                S_.dve(lambda e: e.tensor_copy(out=r_u[PR, :], in_=r_k[PR, :]), reads=[Br], writes=[Br])
                S_.dve(lambda e: e.tensor_tensor(out=r_t[PR, :], in0=r_t[PR, :], in1=r_u[PR, :], op=ALU.subtract), reads=[Br], writes=[Br])
                if which == 0:
                    S_.act(lambda e, cs=cs: e.activation(out=ropeC[PR, cs], in_=r_t[PR, :], func=AF.Sin, scale=2.0 * math.pi),
                           reads=[Br], writes=[Br])
                else:
                    S_.act(lambda e, cs=cs: e.activation(out=ropeS[PR, cs], in_=r_t[PR, :], func=AF.Sin, scale=cst[PR, 1:2]),
                           reads=[Br, B_const], writes=[Br])

        if stop == 0.3:
            finish([])
            return nc
        cast_i = [0]

        def load_cast(dst_ap_fn, src_ap, ncols, dst_buf, stg=None, Bstg=None):
            stg = stg32 if stg is None else stg
            Bstg = Bstg32 if Bstg is None else Bstg
            for c0 in range(0, ncols, 512):
                c1 = min(ncols, c0 + 512)
                i = cast_i[0] % 2
                cast_i[0] += 1
                S_.dma(f"wl{i}", lambda e, i=i, c0=c0, c1=c1: e.dma_start(out=stg[i][:, 0:c1 - c0], in_=src_ap[:, c0:c1]),
                       writes=[Bstg[i]])
                if i == 0:
                    S_.dve(lambda e, i=i, c0=c0, c1=c1: e.tensor_copy(out=dst_ap_fn(c0, c1), in_=stg[i][:, 0:c1 - c0]),
                           reads=[Bstg[i]], writes=[dst_buf])
                else:
                    S_.act(lambda e, i=i, c0=c0, c1=c1: e.copy(out=dst_ap_fn(c0, c1), in_=stg[i][:, 0:c1 - c0]),
                           reads=[Bstg[i]], writes=[dst_buf])

        Bw = Buf("weights")
        for kc in range(8):
            load_cast(lambda c0, c1, kc=kc: w_in3[:, kc, c0:c1], w_in_d[kc * 128:(kc + 1) * 128, :], IN_W, Bw)
            load_cast(lambda c0, c1, kc=kc: w_insw3[:, kc, c0:c1], w_insw_d[kc * 128:(kc + 1) * 128, :], 96, Bw)
        for kc in range(2):
            load_cast(lambda c0, c1, kc=kc: w_qb3[:, kc, c0:c1], w_qb_d[kc * 128:(kc + 1) * 128, :], 768, Bw)
            load_cast(lambda c0, c1, kc=kc: w_qbsw3[:, kc, c0:c1], w_qbsw_d[kc * 128:(kc + 1) * 128, :], 768, Bw)
            load_cast(lambda c0, c1, kc=kc: w_kvb3[:, kc, c0:c1], w_kvb_d[kc * 128:(kc + 1) * 128, :], 1024, Bw)

        if stop == 0:
            finish([])
            return nc
        S_.barrier()
        A.release(m_p1)
        xt = [A.alloc(1024, F32) for _ in range(2)]
        Bxt = [Buf() for _ in range(2)]
        xb = [A.alloc(1024, BF16) for _ in range(2)]
        Bxb = [Buf() for _ in range(2)]
        xT = A.alloc(8 * 512, BF16)
        xT3 = v3(xT, 8)
        BxT = Buf()
        c32 = A.alloc(2 * 512, F32)
        c323 = v3(c32, 2)
        Bc32 = Buf()
        sq = A.alloc(2 * 512, BF16)
        sq3 = v3(sq, 2)
        Bsq = Buf()
        rr = A.alloc(512, F32)
        Brr = Buf()
        kt1 = A.alloc(512, F32)
        kt2 = A.alloc(512, F32)
        Bkt = Buf()
        ks2 = A.alloc(2 * 1536, BF16)
        ks23 = v3(ks2, 2)
        vs = A.alloc(12 * 256, BF16)
        vs4 = vs.rearrange("p (t g d) -> p t g d", t=12, g=2)
        qz = A.alloc(4096, BF16)
        qz5 = qz.rearrange("p (g b s q) -> p g b s q", g=2, b=4, s=4)
        Bks = [Buf() for _ in range(3)]
        Bvs = [Buf() for _ in range(3)]
        Bqs = [Buf()] * 2
        pq_i = A.alloc(768, I32)
        pq_f = A.alloc(768, F32)
        Bpq = Buf()
        Dt = [A.alloc(3 * 128, BF16) for _ in range(2)]
        Dt3 = [v3(d, 3) for d in Dt]
        BDt = [Buf() for _ in range(2)]
        dtmp = A.alloc(128, F32)
        Bdtmp = Buf()
        Psb = [A.alloc(3 * 512, BF16) for _ in range(2)]
        Psb3 = [v3(p, 3) for p in Psb]
        BPsb = [Buf() for _ in range(2)]
        rec = [A.alloc(512, F32) for _ in range(2)]
        Brec = [Buf() for _ in range(2)]

        S_.pool(lambda e: e.memset(vs, 1.0), writes=Bvs)
        S_.pool(lambda e: e.memset(qz, 0.0), writes=[Bqs[0]])
        esrow = A.alloc(1024, BF16)
        zo = A.alloc(128, BF16)
        ntmp = A.alloc(128, F32)
        S_.dve(lambda e: e.memset(zo[0:1, 0:64], 0.0), writes=[B_const])
        S_.dve(lambda e: e.memset(zo[0:1, 64:128], 1.0), writes=[B_const])
        for g in range(2):
            for slot in range(4):
                par, j = slot // 2, slot % 2
                h = 4 * g + 2 * j + par
                S_.dve(lambda e, g=g, slot=slot, h=h: e.tensor_copy(out=esrow[0:1, g * 512 + slot * 128:g * 512 + (slot + 1) * 128],
                                                                    in_=esink[0:1, h:h + 1].to_broadcast([1, 128])),
                       reads=[B_const], writes=[B_const])

        ev_i = [0]

        act_only = [True]

        def evac(out, in_, reads, writes):
            ev_i[0] += 1
            if act_only[0] or ev_i[0] % 2:
                return S_.act(lambda e: e.copy(out=out, in_=in_), reads=reads, writes=writes)
            return S_.dve(lambda e: e.tensor_copy(out=out, in_=in_), reads=reads, writes=writes)

        pbank = [0]

        def nextbank():
            pbank[0] = (pbank[0] + 1) % 3
            return pbank[0]

        def swa_units(c):
            units_ = []
            cs0 = c * 512

            lo = max(0, cs0 - 128)
            hi = min(S, cs0 + 640)

            def load_pq():
                S_.dma("pq", lambda e: e.dma_start(out=pq_i[:, 0:hi - lo], in_=pos_d[0:1, lo:hi].partition_broadcast(128)), writes=[Bpq])
                S_.dve(lambda e: e.tensor_copy(out=pq_f[:, 0:hi - lo], in_=pq_i[:, 0:hi - lo]), reads=[Bpq], writes=[Bpq])

            for qb in range(4):
                i = c * 4 + qb
                offs = [o for o in (-1, 0, 1) if 0 <= i + o < NT]
                no = len(offs)
                di = i % 2
                qcol = (c % 2) * 512 + qb * 128

                def dtiles(i=i, qb=qb, offs=offs, di=di):
                    for n, o in enumerate(offs):
                        kb = i + o
                        pks = pq_f[:, kb * 128 - lo:kb * 128 - lo + 128]
                        if o == 0:
                            S_.act(lambda e, n=n, pks=pks: e.activation(out=Dt3[di][:, n, :], in_=pks, func=AF.Abs, bias=npcol[:, i:i + 1]),
                                   reads=[Bpq, B_const], writes=[BDt[di]])
                        else:
                            mi = 1 if o == -1 else 0
                            S_.act(lambda e, pks=pks: e.activation(out=dtmp, in_=pks, func=AF.Abs, bias=npcol[:, i:i + 1]),
                                   reads=[Bpq, B_const], writes=[Bdtmp])
                            S_.pool(lambda e, n=n, mi=mi: e.tensor_tensor(out=Dt3[di][:, n, :], in0=dtmp, in1=masks3[:, mi, :], op=ALU.add),
                                    reads=[Bdtmp, B_const], writes=[BDt[di]])

                for g in range(2):
                    pi = (i * 2 + g) % 2
                    pvb = (7, 3)[(i * 2 + g) % 2]

                    def A_(i=i, g=g, qb=qb, offs=offs, no=no, di=di, qcol=qcol, pi=pi, first=(qb == 0 and g == 0), dt=(dtiles if g == 0 else None)):
                        if first:
                            load_pq()
                        if dt is not None:
                            dt()
                        for n, o in enumerate(offs):
                            kb = i + o
                            kcol = (kb % 12) * 128
                            kring = (kb // 4) % 3
                            mm(bank(4 + n), ks23[:, g, kcol:kcol + 128], qz5[:, g, qb, :, :].rearrange("p s q -> p (s q)"), True, False,
                               reads=[Bks[kring], Bqs[0]], writes=[PB[4 + n]])
                            mm(bank(4 + n), Dt3[di][:, n, :], dg[:, g * 512:(g + 1) * 512], False, True,
                               reads=[BDt[di], B_const], writes=[PB[4 + n]])
                        S_.act(lambda e: e.activation(out=Psb[pi][:, 0:no * 512], in_=bank(4, no), func=AF.Exp, scale=0.125),
                               reads=[PB[4 + n] for n in range(no)], writes=[BPsb[pi]])

                    def B_(i=i, g=g, offs=offs, no=no, pi=pi, pvb=pvb):
                        for n, o in enumerate(offs):
                            kb = i + o
                            mm(bank(pvb), vs4[:, kb % 12, g, :], Psb3[pi][:, n, :], n == 0, n == no - 1,
                               reads=[Bvs[(kb // 4) % 3], BPsb[pi]], writes=[PB[pvb]])
                            if n == 0:
                                mm(bank(pvb), zo[0:1, :], esrow[0:1, g * 512:(g + 1) * 512], False, False,
                                   reads=[B_const], writes=[PB[pvb]])
                        S_.act(lambda e: e.activation(out=rec[pi][0:64, :], in_=bank(pvb)[64:128, :], func=AF.Ln),
                               reads=[PB[pvb]], writes=[Brec[pi]])
                        S_.act(lambda e: e.activation(out=rec[pi][0:64, :], in_=rec[pi][0:64, :], func=AF.Exp, scale=-1.0),
                               reads=[Brec[pi]], writes=[Brec[pi]])
                        for par in range(2):
                            S_.dve(lambda e, par=par: e.tensor_tensor(
                                out=OTs3[par * 64:par * 64 + 64, 2 * g:2 * g + 2, i * 128:(i + 1) * 128],
                                in0=v3(bank(pvb)[0:64, par * 256:(par + 1) * 256], 2),
                                in1=v3(rec[pi][0:64, par * 256:(par + 1) * 256], 2), op=ALU.mult),
                                reads=[PB[pvb], Brec[pi]])

                    units_.append((A_, B_))
            return units_

        def latent_pieces(c):
            cs = slice(c * 512, (c + 1) * 512)
            pcs = []
            for which, (col0, dst3, gc) in enumerate(((0, cqn3, 0), (256, ckvn3, 2))):
                def lat(m, col0=col0):
                    b = nextbank()
                    for kc in range(8):
                        mm(bank(b), w_in3[:, kc, col0 + m * 128:col0 + (m + 1) * 128], xT3[:, kc, :], kc == 0, kc == 7,
                           reads=[BxT, Bw], writes=[PB[b]])
                    S_.act(lambda e: e.copy(out=c323[:, m, :], in_=bank(b)), reads=[PB[b]], writes=[Bc32])
                    S_.pool(lambda e: e.tensor_tensor(out=sq3[:, m, :], in0=c323[:, m, :], in1=c323[:, m, :], op=ALU.mult),
                            reads=[Bc32], writes=[Bsq])

                def norm(dst3=dst3, gc=gc):
                    b = nextbank()
                    for m in range(2):
                        mm(bank(b), onesb, sq3[:, m, :], m == 0, m == 1, reads=[Bsq, B_const], writes=[PB[b]])
                    S_.act(lambda e: e.activation(out=rr, in_=bank(b), func=AF.Ln, scale=1.0 / 256.0, bias=cst[:, 2:3]),
                           reads=[PB[b], B_const], writes=[Brr])
                    S_.act(lambda e: e.activation(out=rr, in_=rr, func=AF.Exp, scale=-0.5), reads=[Brr], writes=[Brr])
                    for m in range(2):
                        S_.dve(lambda e, m=m: e.scalar_tensor_tensor(
                            out=dst3[:, m, cs], in0=c323[:, m, :], scalar=gcol[:, gc + m:gc + m + 1], in1=rr, op0=ALU.mult, op1=ALU.mult),
                            reads=[Bc32, Brr, B_const])

                pcs.append(lambda lat=lat: lat(0))
                pcs.append(lambda lat=lat: lat(1))
                pcs.append(norm)

            def kr1():
                b1 = nextbank()
                for kc in range(8):
                    mm(bank(b1)[0:96, :], w_in3[:, kc, 448:544], xT3[:, kc, :], kc == 0, kc == 7, reads=[BxT, Bw], writes=[PB[b1]])
                S_.dve(lambda e: e.tensor_tensor(out=kt1[PR, :], in0=bank(b1)[PR, :], in1=ropeC[PR, cs], op=ALU.mult),
                       reads=[PB[b1], Br], writes=[Bkt])

            def kr2():
                b2 = nextbank()
                for kc in range(8):
                    mm(bank(b2)[0:96, :], w_insw3[:, kc, :], xT3[:, kc, :], kc == 0, kc == 7, reads=[BxT, Bw], writes=[PB[b2]])
                S_.dve(lambda e: e.tensor_tensor(out=kt2[PR, :], in0=bank(b2)[PR, :], in1=ropeS[PR, cs], op=ALU.mult),
                       reads=[PB[b2], Br], writes=[Bkt])
                S_.dve(lambda e: e.tensor_tensor(out=krope[PR, cs], in0=kt1[PR, :], in1=kt2[PR, :], op=ALU.add), reads=[Bkt])

            pcs.append(kr1)
            pcs.append(kr2)
            qpcs = []
            for p in range(4):
                def qsp(p=p):
                    b = nextbank()
                    for kc in range(8):
                        mm(bank(b), w_in3[:, kc, 544 + p * 128:544 + (p + 1) * 128], xT3[:, kc, :], kc == 0, kc == 7,
                           reads=[BxT, Bw], writes=[PB[b]])
                    g_, j_ = p // 2, p % 2
                    evac(qz5[0:64, g_, :, j_, :], v3(bank(b)[0:64, :], 4), [PB[b]], [Bqs[0]])
                    evac(qz5[64:128, g_, :, 2 + j_, :], v3(bank(b)[64:128, :], 4), [PB[b]], [Bqs[0]])
                qpcs.append(qsp)
            return pcs, qpcs

        for c in range(NCH):
            cs = slice(c * 512, (c + 1) * 512)
            for t in range(4):
                tt = c * 4 + t
                i = tt % 2
                S_.dma(f"x{i}", lambda e, i=i, tt=tt: e.dma_start(out=xt[i], in_=x_d[tt * 128:(tt + 1) * 128, :]), writes=[Bxt[i]])
                if t % 2 == 0:
                    S_.dve(lambda e, i=i: e.tensor_copy(out=xb[i], in_=xt[i]), reads=[Bxt[i]], writes=[Bxb[i]])
                else:
                    S_.act(lambda e, i=i: e.copy(out=xb[i], in_=xt[i]), reads=[Bxt[i]], writes=[Bxb[i]])
                for half in range(2):
                    b = nextbank()
                    for k4 in range(4):
                        kc = half * 4 + k4
                        S_.pe(lambda e, b=b, k4=k4, kc=kc, i=i: e.transpose(out=bankb(b)[:, k4 * 128:(k4 + 1) * 128],
                                                                             in_=xb[i][:, kc * 128:(kc + 1) * 128], identity=identb),
                              reads=[Bxb[i], B_const], writes=[PB[b]])
                    evac(xT3[:, half * 4:half * 4 + 4, t * 128:(t + 1) * 128], v3(bankb(b)[:, 0:512], 4), [PB[b]], [BxT])
            b = nextbank()
            for kc in range(8):
                mm(bank(b), w_in3[:, kc, 1056:1184], xT3[:, kc, :], kc == 0, kc == 7, reads=[BxT, Bw], writes=[PB[b]])
            kr = c % 3
            kc0 = kr * 512
            for g in range(2):
                for dsth in range(2):
                    evac(ks23[dsth * 64:dsth * 64 + 64, g, kc0:kc0 + 512], bank(b)[g * 64:g * 64 + 64, :], [PB[b]], [Bks[kr]])
            b = nextbank()
            for t in range(4):
                for kc in range(8):
                    mm(bank(b)[:, t * 128:(t + 1) * 128], xT3[:, kc, t * 128:(t + 1) * 128], w_in3[:, kc, 1184:1312], kc == 0, kc == 7,
                       reads=[BxT, Bw], writes=[PB[b]])
            t0 = (c * 4) % 12
            evac(vs4[:, t0:t0 + 4, :, 0:64], bank(b).rearrange("p (t g d) -> p t g d", t=4, g=2), [PB[b]], [Bvs[kr]])
            pcs, qpcs = latent_pieces(c)
            us = swa_units(c - 1) if c >= 1 else []
            if not us:
                for p in pcs:
                    p()
            else:
                per = -(-len(pcs) // len(us))
                for k, (A_, B_) in enumerate(us):
                    A_()
                    for p in pcs[k * per:(k + 1) * per]:
                        p()
                    B_()
                for p in pcs[len(us) * per:]:
                    p()
            for p in qpcs:
                p()
        for (A_, B_) in swa_units(NCH - 1):
            A_()
            B_()

        if stop == 1:
            finish([])
            return nc
        act_only[0] = False
        S_.barrier()
        A.release(m_p12)
        KT = [A.alloc(S, BF16) for _ in range(2)]
        BKT = [Buf() for _ in range(2)]
        Vh = [A.alloc(NT * 128, BF16) for _ in range(2)]
        Vh3 = [v3(v, NT) for v in Vh]
        BVh = [Buf() for _ in range(2)]
        QT = [A.alloc(512, BF16) for _ in range(2)]
        BQT = [Buf() for _ in range(2)]
        PT = [A.alloc(3 * 512, BF16) for _ in range(3)]
        PT3 = [v3(p, 3) for p in PT]
        BPT = [Buf() for _ in range(3)]
        rec2 = [A.alloc(512, F32) for _ in range(2)]
        Brec2 = [Buf() for _ in range(2)]
        qt1 = A.alloc(512, F32)
        qt2 = A.alloc(512, F32)
        Bqt = Buf()
        for i in range(2):
            S_.pool(lambda e, i=i: e.memset(Vh[i], 1.0), writes=[BVh[i]])
        SC = 96.0 ** -0.5
        NSTG = 4
        sg32 = [A.alloc(1024, F32) for _ in range(NSTG)]
        sg16 = [A.alloc(1024, BF16) for _ in range(NSTG)]
        Bsg32 = [Buf() for _ in range(NSTG)]
        Bsg16 = [Buf() for _ in range(NSTG)]
        scr_jobs = []
        for kc in range(8):
            for hf in range(2):
                for j0 in range(0, NJ, 8):
                    scr_jobs.append(("up", kc, hf, j0, min(8, NJ - j0)))
        for j in range(NJ):
            scr_jobs.append(("dn", j, 0, 0, 0))
        scr_state = [0, 0]

        def emit_scratch(n):
            for _ in range(n):
                if scr_state[0] >= len(scr_jobs):
                    return
                kind, a0, hf, j0, nj = scr_jobs[scr_state[0]]
                scr_state[0] += 1
                i = scr_state[1] % NSTG
                scr_state[1] += 1
                if kind == "up":
                    kc = a0
                    w = nj * 128
                    c0 = hf * DFF + j0 * 128
                    S_.dma(f"wl{i}", lambda e, i=i, kc=kc, c0=c0, w=w: e.dma_start(out=sg32[i][:, 0:w], in_=w_up_d[kc * 128:(kc + 1) * 128, c0:c0 + w]),
                           writes=[Bsg32[i]])
                    S_.dve(lambda e, i=i, w=w: e.tensor_copy(out=sg16[i][:, 0:w], in_=sg32[i][:, 0:w]), reads=[Bsg32[i]], writes=[Bsg16[i]])
                    S_.dma(f"ws{i}", lambda e, i=i, kc=kc, hf=hf, j0=j0, nj=nj, w=w: e.dma_start(
                        out=wup_s[j0:j0 + nj, :, kc, hf * 128:(hf + 1) * 128].rearrange("j p f -> p j f"), in_=v3(sg16[i][:, 0:w], nj)),
                        reads=[Bsg16[i]])
                else:
                    j = a0
                    S_.dma(f"wl{i}", lambda e, i=i, j=j: e.dma_start(out=sg32[i], in_=w_dn_d[j * 128:(j + 1) * 128, :]), writes=[Bsg32[i]])
                    S_.dve(lambda e, i=i: e.tensor_copy(out=sg16[i], in_=sg32[i]), reads=[Bsg32[i]], writes=[Bsg16[i]])
                    S_.dma(f"ws{i}", lambda e, i=i, j=j: e.dma_start(out=wdn_s[j * 128:(j + 1) * 128, :], in_=sg16[i]), reads=[Bsg16[i]])

        groups = []
        kb = 0
        while kb < NT:
            n = min(3, NT - kb)
            groups.append((kb, n))
            kb += n
        NG = len(groups)

        def devac(out, in_, reads, writes):
            return S_.dve(lambda e: e.tensor_copy(out=out, in_=in_), reads=reads, writes=writes)

        def head_pieces(h, pb):
            hb = h % 2
            pcs = []

            def kpiece(c):
                cs = slice(c * 512, (c + 1) * 512)
                for kc in range(2):
                    mm(bank(pb)[0:64, :], w_kvb3[:, kc, h * 128:h * 128 + 64], ckvn3[:, kc, cs], kc == 0, kc == 1, writes=[PB[pb]])
                devac(KT[hb][0:64, cs], bank(pb)[0:64, :], [PB[pb]], [BKT[hb]])

            def vpiece(t8):
                for t in range(t8, t8 + 8):
                    for kc in range(2):
                        mm(bank(pb)[:, (t - t8) * 64:(t - t8 + 1) * 64], ckvn3[:, kc, t * 128:(t + 1) * 128],
                           w_kvb3[:, kc, h * 128 + 64:h * 128 + 128], kc == 0, kc == 1, writes=[PB[pb]])
                devac(Vh3[hb][:, t8:t8 + 8, 0:64], v3(bank(pb), 8), [PB[pb]], [BVh[hb]])

            pcs.append(lambda: S_.pool(lambda e: e.tensor_copy(out=KT[hb][PR, :], in_=krope[PR, :]), writes=[BKT[hb]]))
            for c in range(NCH):
                pcs.append(lambda c=c: kpiece(c))
            for t8 in range(0, NT, 8):
                pcs.append(lambda t8=t8: vpiece(t8))
            return pcs

        def q_pieces(h, c, pb):
            cs = slice(c * 512, (c + 1) * 512)
            qi = (h * NCH + c) % 2

            def q1():
                for kc in range(2):
                    mm(bank(pb)[0:96, :], w_qb3[:, kc, h * 96:(h + 1) * 96], cqn3[:, kc, cs], kc == 0, kc == 1, writes=[PB[pb]])
                S_.dve(lambda e: e.tensor_copy(out=QT[qi][0:64, :], in_=bank(pb)[0:64, :]), reads=[PB[pb]], writes=[BQT[qi]])
                S_.dve(lambda e: e.tensor_tensor(out=qt1[PR, :], in0=bank(pb)[PR, :], in1=ropeC[PR, cs], op=ALU.mult),
                       reads=[PB[pb]], writes=[Bqt])

            def q2():
                for kc in range(2):
                    mm(bank(pb)[0:96, :], w_qbsw3[:, kc, h * 96:(h + 1) * 96], cqn3[:, kc, cs], kc == 0, kc == 1, writes=[PB[pb]])
                S_.dve(lambda e: e.tensor_tensor(out=qt2[PR, :], in0=bank(pb)[PR, :], in1=ropeS[PR, cs], op=ALU.mult),
                       reads=[PB[pb]], writes=[Bqt])
                S_.dve(lambda e: e.tensor_tensor(out=QT[qi][PR, :], in0=qt1[PR, :], in1=qt2[PR, :], op=ALU.add),
                       reads=[Bqt], writes=[BQT[qi]])

            return [q1, q2]

        units = [(h, c, g) for h in range(8) for c in range(NCH) for g in range(NG)]

        def u_info(idx):
            h, c, g = units[idx]
            kb0, n = groups[g]
            return h, c, g, kb0, n, (idx % 2) * 3, idx % 3, (h * NCH + c) % 2, h % 2

        def emit_qk(idx):
            h, c, g, kb0, n, sb0, pi, qi, hb = u_info(idx)
            for j in range(n):
                kbj = kb0 + j
                mm(bank(sb0 + j), KT[hb][0:96, kbj * 128:(kbj + 1) * 128], QT[qi][0:96, :], True, True,
                   reads=[BKT[hb], BQT[qi]], writes=[PB[sb0 + j]])

        def emit_exp(idx):
            h, c, g, kb0, n, sb0, pi, qi, hb = u_info(idx)
            S_.act(lambda e: e.activation(out=PT[pi][:, 0:n * 512], in_=bank(sb0, n), func=AF.Exp, scale=SC),
                   reads=[PB[sb0 + j] for j in range(n)], writes=[BPT[pi]])

        def emit_pv(idx):
            h, c, g, kb0, n, sb0, pi, qi, hb = u_info(idx)
            ab = 6 + (h * NCH + c) % 2
            for j in range(n):
                kbj = kb0 + j
                mm(bank(ab), Vh3[hb][:, kbj, :], PT3[pi][:, j, :], kbj == 0, kbj == NT - 1,
                   reads=[BVh[hb], BPT[pi]], writes=[PB[ab]])

        def emit_norm(h, c):
            cs = slice(c * 512, (c + 1) * 512)
            m = h * NCH + c
            ab = 6 + m % 2
            ri = m % 2
            par = h % 2
            S_.dve(lambda e: e.reciprocal(out=rec2[ri][0:64, :], in_=bank(ab)[64:128, :]), reads=[PB[ab]], writes=[Brec2[ri]])
            S_.dve(lambda e: e.tensor_tensor(out=OTm3[par * 64:par * 64 + 64, h // 2, cs], in0=bank(ab)[0:64, :],
                                             in1=rec2[ri][0:64, :], op=ALU.mult), reads=[PB[ab], Brec2[ri]])

        hp6, hp7 = head_pieces(0, 6), head_pieces(0, 7)
        for k in range(len(hp6)):
            (hp6 if k % 2 == 0 else hp7)[k]()
        for p in q_pieces(0, 0, 6):
            p()
        emit_qk(0)
        emit_qk(1)
        sched = {}
        for m in range(8 * NCH):
            h, c = m // NCH, m % NCH
            u0 = m * NG
            ob = 6 + (m + 1) % 2
            pcs = []
            if c == NCH - 1 and h + 1 < 8:
                pcs += head_pieces(h + 1, ob)
            if m + 1 < 8 * NCH:
                q1, q2 = q_pieces((m + 1) // NCH, (m + 1) % NCH, ob)
                qslot = (max(0, min(NG - 3, 3)), max(0, min(NG - 3, 6)))
            else:
                q1 = q2 = None
            g0 = min(2, max(0, NG - 3))
            L = max(1, NG - 2 - g0)
            per = -(-len(pcs) // L) if pcs else 0
            for g in range(NG):
                lst = []
                if q1 is not None and g == qslot[0]:
                    lst.append(q1)
                k = g - g0
                if per and 0 <= k < L:
                    lst += pcs[k * per:(k + 1) * per] if k < L - 1 else pcs[k * per:]
                if q2 is not None and g == qslot[1]:
                    lst.append(q2)
                sched[u0 + g] = lst
        for idx in range(len(units)):
            h, c, g = units[idx]
            for p in sched.get(idx, []):
                p()
            if g == 5 and not (h == 0 and c == 0):
                emit_scratch(2)
            emit_exp(idx)
            if idx + 2 < len(units):
                emit_qk(idx + 2)
            emit_pv(idx)
            if g == NG - 1:
                emit_norm(h, c)
        emit_scratch(10 ** 6)

        if stop == 2:
            finish([])
            return nc
        S_.barrier()
        A.release(m_p3)
        lnp = A.alloc(4 * D, F32)
        lnp3 = v3(lnp, 4)
        w_o = A.alloc(8 * D, BF16)
        w_o3 = v3(w_o, 8)
        xres = [A.alloc(D, F32) for _ in range(2)]
        Bxres = [Buf() for _ in range(2)]
        NX1 = 9
        x1 = [A.alloc(D, F32) for _ in range(NX1)]
        Bx1 = [Buf() for _ in range(NX1)]
        x1T = A.alloc(8 * 1024, BF16)
        x1T3 = v3(x1T, 8)
        Bx1T = [Buf() for _ in range(8)]
        AT = A.alloc(NJ * 512, BF16)
        AT3 = v3(AT, NJ)
        BAT = [Buf() for _ in range(NJ)]
        wub = [A.alloc(8 * 256, BF16) for _ in range(2)]
        wub3 = [v3(w, 8) for w in wub]
        Bwub = [Buf() for _ in range(2)]
        NWD = 4
        wdb = [A.alloc(D, BF16) for _ in range(NWD)]
        Bwdb = [Buf() for _ in range(NWD)]
        hacc = [A.alloc(512, F32) for _ in range(4)]
        Bhacc = [Buf() for _ in range(4)]
        stats = [A.alloc(32, F32) for _ in range(4)]
        Bstats = [Buf() for _ in range(4)]
        ln_i = [0]
        hbuf = A.alloc(16, BF16)
        hbuf3 = v3(hbuf, 8)
        Bhb = Buf()
        Blnp = Buf()
        for k in range(4):
            S_.dma("c3", lambda e, k=k: e.dma_start(out=lnp3[:, k, :], in_=ln_d[k:k + 1, :].partition_broadcast(128)), writes=[Blnp])
        Bwo = Buf()
        stg3 = [hacc[0], hacc[1]]
        Bstg3 = [Bhacc[0], Bhacc[1]]
        for kc in range(8):
            load_cast(lambda c0, c1, kc=kc: w_o3[:, kc, c0:c1], w_o_d[kc * 128:(kc + 1) * 128, :], D, Bwo, stg3, Bstg3)

        def layer_norm(buf, Bb, gk):
            k = ln_i[0] % 4
            ln_i[0] += 1
            stat, Bstat = stats[k], Bstats[k]
            for hh in range(2):
                S_.dve(lambda e, hh=hh: e.bn_stats(out=stat[:, hh * 6:(hh + 1) * 6], in_=buf[:, hh * 512:(hh + 1) * 512]),
                       reads=[Bb], writes=[Bstat])
            S_.dve(lambda e: e.bn_aggr(out=stat[:, 16:18], in_=stat[:, 0:12]), reads=[Bstat], writes=[Bstat])
            S_.act(lambda e: e.activation(out=stat[:, 18:19], in_=stat[:, 17:18], func=AF.Sqrt, bias=cst[:, 3:4]),
                   reads=[Bstat, B_const], writes=[Bstat])
            S_.dve(lambda e: e.reciprocal(out=stat[:, 19:20], in_=stat[:, 18:19]), reads=[Bstat], writes=[Bstat])
            S_.dve(lambda e: e.scalar_tensor_tensor(out=stat[:, 20:21], in0=stat[:, 16:17], scalar=-1.0, in1=stat[:, 19:20],
                                                    op0=ALU.mult, op1=ALU.mult), reads=[Bstat], writes=[Bstat])
            S_.act(lambda e: e.activation(out=buf, in_=buf, func=AF.Identity, scale=stat[:, 19:20], bias=stat[:, 20:21]),
                   reads=[Bb, Bstat], writes=[Bb])
            S_.dve(lambda e: e.tensor_tensor(out=buf, in0=buf, in1=lnp3[:, gk, :], op=ALU.mult), reads=[Bb, Blnp], writes=[Bb])
            return S_.pool(lambda e: e.tensor_tensor(out=buf, in0=buf, in1=lnp3[:, gk + 1, :], op=ALU.add), reads=[Bb, Blnp], writes=[Bb])

        def post_a1(t, b0):
            xi = t % 2
            s = t % NX1
            S_.dma(f"x{xi}", lambda e: e.dma_start(out=xres[xi], in_=x_d[t * 128:(t + 1) * 128, :]), writes=[Bxres[xi]])
            for nh in range(2):
                for kp in range(8):
                    src = OTm3[:, kp, t * 128:(t + 1) * 128] if kp < 4 else OTs3[:, kp - 4, t * 128:(t + 1) * 128]
                    mm(bank(b0 + nh), src, w_o3[:, kp, nh * 512:(nh + 1) * 512], kp == 0, kp == 7, reads=[Bwo], writes=[PB[b0 + nh]])
                S_.dve(lambda e, nh=nh: e.scalar_tensor_tensor(out=x1[s][:, nh * 512:(nh + 1) * 512], in0=xres[xi][:, nh * 512:(nh + 1) * 512],
                                                               scalar=ALPHA, in1=bank(b0 + nh), op0=ALU.mult, op1=ALU.add),
                       reads=[Bxres[xi], PB[b0 + nh]], writes=[Bx1[s]])

        def post_a1_ln(t):
            s = t % NX1
            layer_norm(x1[s], Bx1[s], 0)

        def post_a2(t, b0):
            s = t % NX1
            rc = (t % 8) * 128
            for half in range(2):
                b = b0 + half
                for k4 in range(4):
                    kc = half * 4 + k4
                    S_.pe(lambda e, b=b, k4=k4, kc=kc: e.transpose(out=bank(b)[:, k4 * 128:(k4 + 1) * 128], in_=x1[s][:, kc * 128:(kc + 1) * 128],
                                                                   identity=ident32), reads=[Bx1[s], B_const], writes=[PB[b]])
                evac(x1T3[:, half * 4:half * 4 + 4, rc:rc + 128], v3(bank(b), 4), [PB[b]], [Bx1T[t % 8]])

        out_ops = []
        deferred = []

        def ffn(c, mid=None, post=None):
            base = (c % 2) * 512
            right = (base + 512) % 1024
            rd = [Bx1T[(4 * c + k) % 8] for k in range(4)]
            rdh = [Bx1T[(4 * c - 1) % 8], Bx1T[(4 * c + 4) % 8]]
            has_l, has_r = c > 0, c < NCH - 1

            def load_dn(j):
                i = (c * NJ + j) % NWD
                S_.dma(f"wd{i}", lambda e: e.dma_start(out=wdb[i], in_=wdn_s[j * 128:(j + 1) * 128, :]), writes=[Bwdb[i]])

            def load_up(j):
                i = (c * NJ + j) % 2
                S_.dma(f"wu{i}", lambda e: e.dma_start(out=wub3[i], in_=wup_s[j]), writes=[Bwub[i]])

            load_up(0)
            left = (base - 1) % 1024
            S_.pool(lambda e: e.tensor_copy(out=hbuf3[:, :, 0:1], in_=x1T3[:, :, right:right + 1]), reads=rdh, writes=[Bhb])
            S_.pool(lambda e: e.tensor_copy(out=hbuf3[:, :, 1:2], in_=x1T3[:, :, left:left + 1]), reads=rdh, writes=[Bhb])
            for j in range(NJ):
                if j + 1 < NJ:
                    load_up(j + 1)
                if j == NJ - 4:
                    for jj in range(NWD - 1):
                        load_dn(jj)
                if j == 6:
                    while deferred:
                        deferred.pop(0)()
                wi = (c * NJ + j) % 2
                accs = []
                for half in range(2):
                    f = half * NJ + j
                    hb_ = (j * 2 + half) % 6
                    xb_ = 6 + half
                    ai = (j * 2 + half) % 4
                    for kc in range(8):
                        mm(bank(hb_), wub3[wi][:, kc, half * 128:(half + 1) * 128], x1T3[:, kc, base:base + 512], kc == 0, kc == 7,
                           reads=[Bwub[wi]] + rd, writes=[PB[hb_]])
                    if has_l or has_r:
                        for kc in range(8):
                            mm(bank(xb_)[:, 0:2], wub3[wi][:, kc, half * 128:(half + 1) * 128],
                               hbuf3[:, kc, :], kc == 0, kc == 7, reads=[Bwub[wi], Bhb], writes=[PB[xb_]])
                    w0 = convp[:, 0 * 44 + f:0 * 44 + f + 1]
                    w1 = convp[:, 1 * 44 + f:1 * 44 + f + 1]
                    w2 = convp[:, 2 * 44 + f:2 * 44 + f + 1]
                    bb = convp[:, 3 * 44 + f:3 * 44 + f + 1]
                    acc = hacc[ai]
                    S_.act(lambda e, acc=acc, hb_=hb_, w1=w1, bb=bb: e.activation(out=acc, in_=bank(hb_), func=AF.Identity, scale=w1, bias=bb),
                           reads=[PB[hb_], B_const], writes=[Bhacc[ai]])
                    S_.dve(lambda e, acc=acc, hb_=hb_, w0=w0: e.scalar_tensor_tensor(out=acc[:, 1:512], in0=bank(hb_)[:, 0:511], scalar=w0,
                                                                                     in1=acc[:, 1:512], op0=ALU.mult, op1=ALU.add),
                           reads=[PB[hb_], Bhacc[ai], B_const], writes=[Bhacc[ai]])
                    S_.dve(lambda e, acc=acc, hb_=hb_, w2=w2: e.scalar_tensor_tensor(out=acc[:, 0:511], in0=bank(hb_)[:, 1:512], scalar=w2,
                                                                                     in1=acc[:, 0:511], op0=ALU.mult, op1=ALU.add),
                           reads=[PB[hb_], Bhacc[ai], B_const], writes=[Bhacc[ai]])
                    if has_l:
                        S_.dve(lambda e, acc=acc, xb_=xb_, w0=w0: e.scalar_tensor_tensor(out=acc[:, 0:1], in0=bank(xb_)[:, 1:2], scalar=w0,
                                                                                         in1=acc[:, 0:1], op0=ALU.mult, op1=ALU.add),
                               reads=[PB[xb_], Bhacc[ai], B_const], writes=[Bhacc[ai]])
                    if has_r:
                        S_.dve(lambda e, acc=acc, xb_=xb_, w2=w2: e.scalar_tensor_tensor(out=acc[:, 511:512], in0=bank(xb_)[:, 0:1], scalar=w2,
                                                                                         in1=acc[:, 511:512], op0=ALU.mult, op1=ALU.add),
                               reads=[PB[xb_], Bhacc[ai], B_const], writes=[Bhacc[ai]])
                    accs.append((acc, Bhacc[ai]))
                (ag, Bg), (au, Bu) = accs
                S_.act(lambda e, ag=ag: e.activation(out=ag, in_=ag, func=AF.Gelu), reads=[Bg], writes=[Bg])
                S_.pool(lambda e, ag=ag, au=au, j=j: e.tensor_tensor(out=AT3[:, j, :], in0=ag, in1=au, op=ALU.mult),
                        reads=[Bg, Bu], writes=[BAT[j]])
            if mid is not None:
                mid()
            for j in range(NJ):
                if j + NWD - 1 < NJ:
                    load_dn(j + NWD - 1)
                wi = (c * NJ + j) % NWD
                for tl in range(4):
                    for nh in range(2):
                        b = tl * 2 + nh
                        mm(bank(b), AT3[:, j, tl * 128:(tl + 1) * 128], wdb[wi][:, nh * 512:(nh + 1) * 512], j == 0, j == NJ - 1,
                           reads=[BAT[j], Bwdb[wi]], writes=[PB[b]])
            for tl in range(4):
                t = c * 4 + tl
                s = t % NX1
                for nh in range(2):
                    b = tl * 2 + nh
                    S_.dve(lambda e, s=s, nh=nh, b=b: e.scalar_tensor_tensor(out=x1[s][:, nh * 512:(nh + 1) * 512],
                                                                             in0=x1[s][:, nh * 512:(nh + 1) * 512], scalar=ALPHA,
                                                                             in1=bank(b), op0=ALU.mult, op1=ALU.add),
                           reads=[PB[b], Bx1[s]], writes=[Bx1[s]])
            if post is not None:
                post()
            for tl in range(4):
                t = c * 4 + tl
                s = t % NX1
                layer_norm(x1[s], Bx1[s], 2)
                deferred.append(lambda s=s, t=t: out_ops.append(
                    S_.dma(f"o{t % 4}", lambda e: e.dma_start(out=out_d[t * 128:(t + 1) * 128, :], in_=x1[s]), reads=[Bx1[s]])))

        for k, t in enumerate(range(min(NT, 5))):
            post_a1(t, (k % 4) * 2)
        for k, t in enumerate(range(min(NT, 5))):
            post_a1_ln(t)
        for k, t in enumerate(range(min(NT, 5))):
            post_a2(t, (k % 4) * 2)
        for c in range(NCH):
            nxt = list(range(4 * c + 5, min(NT, 4 * c + 9)))

            def mid(nxt=nxt):
                for k, t in enumerate(nxt):
                    post_a1(t, k * 2)
                for k, t in enumerate(nxt):
                    post_a1_ln(t)

            def post(nxt=nxt):
                for k, t in enumerate(nxt):
                    post_a2(t, k * 2)

            ffn(c, mid, post)

        while deferred:
            deferred.pop(0)()
        finish(out_ops[-4:])
    return nc


def _consts():
    inv_freq = (10000.0 ** (-np.arange(0, 32, 2, dtype=np.float32) / 32.0)).astype(np.float32)
    cst = np.zeros((128, 8), np.float32)
    for p in range(64, 96):
        j = (p - 64) % 16
        cst[p, 0] = inv_freq[j] / (2.0 * math.pi)
        cst[p, 1] = (-1.0 if p < 80 else 1.0) * 2.0 * math.pi
    cst[:, 2] = RMS_EPS
    cst[:, 3] = LN_EPS
    slopes = (2.0 ** (-8.0 * (np.arange(8, dtype=np.float32) + 1.0) / 8.0)).astype(np.float32)
    slp = np.broadcast_to((-8.0 * slopes)[None, :], (128, 8)).astype(np.float32).copy()
    k = np.arange(128)[:, None]
    q = np.arange(128)[None, :]
    masks = np.zeros((2, 128, 128), np.float32)
    masks[0] = np.where(k >= q, 0.0, BIGM)
    masks[1] = np.where(k <= q, 0.0, BIGM)
    return cst, slp, masks, np.eye(128, dtype=np.float32)


def make_in_maps(S, x, positions, w_in, q_norm_g, w_q_b, kv_norm_g, w_kv_b, swa_sinks, w_o, ln1_g, ln1_b, w_up, conv_w,
                 conv_b, w_down, ln2_g, ln2_b):
    f = lambda a: np.ascontiguousarray(np.asarray(a, dtype=np.float32))
    cst, slp, masks, ident = _consts()
    perm = np.concatenate([np.arange(16, 32), np.arange(0, 16)])
    w_in = f(w_in)
    w_insw = np.concatenate([w_in[:, 448:512], w_in[:, 512 + perm]], axis=1)
    w_q_b = f(w_q_b)
    w_qbsw = w_q_b.copy()
    for h in range(8):
        w_qbsw[:, h * 96 + 64:h * 96 + 96] = w_q_b[:, h * 96 + 64 + perm]
    graw = np.stack([f(q_norm_g)[0:128], f(q_norm_g)[128:256], f(kv_norm_g)[0:128], f(kv_norm_g)[128:256]])
    lnp = np.stack([f(ln1_g), f(ln1_b), f(ln2_g), f(ln2_b)])
    cw = f(conv_w).reshape(3, 2 * DFF)
    convp = np.stack([cw[0].reshape(44, 128), cw[1].reshape(44, 128), cw[2].reshape(44, 128), f(conv_b).reshape(44, 128)])
    shared = {
        "w_in": w_in, "w_insw": f(w_insw), "w_qb": w_q_b, "w_qbsw": f(w_qbsw), "w_kvb": f(w_kv_b), "w_o": f(w_o),
        "w_up": f(w_up), "w_down": f(w_down), "graw": f(graw), "sinks": f(swa_sinks).reshape(1, 8), "lnp": f(lnp),
        "convp": f(convp), "ident": ident, "cst": cst, "masks": masks, "slopes": slp,
    }
    x = np.asarray(x, dtype=np.float32)
    positions = np.asarray(positions, dtype=np.int32)
    maps = []
    for b in range(x.shape[0]):
        m = dict(shared)
        m["x"] = np.ascontiguousarray(x[b])
        m["pos"] = np.ascontiguousarray(positions[b].reshape(1, S))
        m["posr"] = np.ascontiguousarray(positions[b].reshape(S // 128, 128))
        maps.append(m)
    return maps


_NC_CACHE = {}


def kernel(**inputs):
    x = np.asarray(inputs["x"])
    B, S, _ = x.shape
    if S not in _NC_CACHE:
        _NC_CACHE[S] = build(S)
    nc = _NC_CACHE[S]
    maps = make_in_maps(S, **inputs)
    res = run_bass_kernel_spmd(nc, maps, core_ids=list(range(B)))
    return np.stack([np.asarray(r["out"], dtype=np.float32) for r in res.results], axis=0)
```

```python
import contextlib
import math
import numpy as np
import concourse.bass as bass
import concourse.mybir as mybir
from concourse.bass_utils import run_bass_kernel_spmd

F32 = mybir.dt.float32
BF16 = mybir.dt.bfloat16
I32 = mybir.dt.int32
AF = mybir.ActivationFunctionType
ALU = mybir.AluOpType

D = 1024
DFF = 2816
NJ = 22
IN_W = 1312
ALPHA = 2.0 ** 0.25
LN_EPS = 1e-5
RMS_EPS = 1e-6
BIGM = 32768.0
ENGS = ("pe", "act", "dve", "pool", "sp")


class Op:
    __slots__ = ("eng", "fn", "deps", "needed", "seq", "key", "is_dma", "idx")

    def __init__(self, eng, fn, key, is_dma):
        self.eng = eng
        self.fn = fn
        self.deps = {}
        self.needed = False
        self.seq = 0
        self.key = key
        self.is_dma = is_dma
        self.idx = 0


class Buf:
    __slots__ = ("name", "writers", "readers")

    def __init__(self, name=""):
        self.name = name
        self.writers = {}
        self.readers = {}


class Sched:
    def __init__(self, nc):
        self.nc = nc
        self.ops = {e: [] for e in ENGS}
        self.stream_cnt = {}
        self.last = {}
        self.bar = {}

    def _add_dep(self, op, d):
        if d is None or d is op:
            return
        if d.key == op.key and op.eng == "pe" and not op.is_dma:
            return
        cur = op.deps.get(d.key)
        if cur is None or d.idx > cur.idx:
            op.deps[d.key] = d

    def barrier(self):
        self.bar = dict(self.last)

    def op(self, eng, fn, reads=(), writes=(), deps=(), stream=None):
        is_dma = stream is not None
        key = stream if is_dma else eng
        o = Op(eng, fn, key, is_dma)
        if is_dma:
            self.stream_cnt[stream] = self.stream_cnt.get(stream, 0) + 1
            o.idx = self.stream_cnt[stream]
        else:
            o.idx = len(self.ops[eng]) + 1
        for d in self.bar.values():
            self._add_dep(o, d)
        for d in deps:
            self._add_dep(o, d)
        for b in reads:
            for w in b.writers.values():
                self._add_dep(o, w)
        for b in writes:
            for r in b.readers.values():
                self._add_dep(o, r)
            for w in b.writers.values():
                self._add_dep(o, w)
        for b in reads:
            b.readers[key] = o
        for b in writes:
            if b.readers:
                b.readers = {}
                b.writers = {}
            b.writers[key] = o
        for d in o.deps.values():
            d.needed = True
        self.ops[eng].append(o)
        self.last[key] = o
        return o

    def pe(self, fn, **kw):
        return self.op("pe", fn, **kw)

    def act(self, fn, **kw):
        return self.op("act", fn, **kw)

    def dve(self, fn, **kw):
        return self.op("dve", fn, **kw)

    def pool(self, fn, **kw):
        return self.op("pool", fn, **kw)

    def dma(self, stream, fn, eng="sp", **kw):
        return self.op(eng, fn, stream=stream, **kw)

    def emit(self, final_waits=()):
        nc = self.nc
        for e in ENGS:
            n = 0
            for o in self.ops[e]:
                if not o.is_dma and o.needed:
                    n += 1
                    o.seq = n
        with contextlib.ExitStack() as st:
            sems = {}
            for e in ENGS:
                sems[e] = st.enter_context(nc.semaphore("s_" + e))
            for s in self.stream_cnt:
                sems[s] = st.enter_context(nc.semaphore("d_" + s))
            block = st.enter_context(nc.Block())

            def run(e, engobj):
                waited = {}

                def wait(d):
                    val = 16 * d.idx if d.is_dma else d.seq
                    if waited.get(d.key, 0) < val:
                        engobj.wait_ge(sems[d.key], val)
                        waited[d.key] = val

                for o in self.ops[e]:
                    for d in o.deps.values():
                        wait(d)
                    ins = o.fn(engobj)
                    if o.is_dma:
                        ins.then_inc(sems[o.key], 16)
                    elif o.needed:
                        ins.then_inc(sems[o.key], 1)
                if e == "sp":
                    for d in final_waits:
                        wait(d)

            @block.tensor
            def _(e):
                run("pe", e)

            @block.scalar
            def _(e):
                run("act", e)

            @block.vector
            def _(e):
                run("dve", e)

            @block.gpsimd
            def _(e):
                run("pool", e)

            @block.sync
            def _(e):
                run("sp", e)


class _Done(Exception):
    pass


class Arena:
    def __init__(self, ap, n):
        self.ap = ap
        self.n = n
        self.off = 0

    def alloc(self, cols, dt):
        units = cols * (2 if dt in (F32, I32) else 1)
        units = (units + 15) // 16 * 16
        a = self.ap[:, self.off:self.off + units]
        self.off += units
        assert self.off <= self.n, f"arena overflow {self.off} > {self.n}"
        if dt != BF16:
            a = a.bitcast(dt)
        return a[:, 0:cols]

    def mark(self):
        return self.off

    def release(self, m):
        self.off = m


def v3(ap, a):
    return ap.rearrange("p (a b) -> p a b", a=a)


def build(S, dbg=None, stop=None):
    NCH = S // 512
    NT = S // 128
    nc = bass.Bass("TRN2", target_bir_lowering=False)

    def din(name, shape, dt):
        return nc.dram_tensor(name, shape, dt, kind="ExternalInput").ap()

    x_d = din("x", [S, D], F32)
    pos_d = din("pos", [1, S], I32)
    posr_d = din("posr", [NT, 128], I32)
    w_in_d = din("w_in", [D, IN_W], F32)
    w_insw_d = din("w_insw", [D, 96], F32)
    w_qb_d = din("w_qb", [256, 768], F32)
    w_qbsw_d = din("w_qbsw", [256, 768], F32)
    w_kvb_d = din("w_kvb", [256, 1024], F32)
    w_o_d = din("w_o", [D, D], F32)
    w_up_d = din("w_up", [D, 2 * DFF], F32)
    w_dn_d = din("w_down", [DFF, D], F32)
    graw_d = din("graw", [4, 128], F32)
    sinks_d = din("sinks", [1, 8], F32)
    ln_d = din("lnp", [4, D], F32)
    convp_d = din("convp", [4, 44, 128], F32)
    ident_d = din("ident", [128, 128], F32)
    cst_d = din("cst", [128, 8], F32)
    masks_d = din("masks", [2, 128, 128], F32)
    slopes_d = din("slopes", [128, 8], F32)
    out_d = nc.dram_tensor("out", [S, D], F32, kind="ExternalOutput").ap()
    wup_s = nc.dram_tensor("wup_s", [NJ, 128, 8, 256], BF16).ap()
    wdn_s = nc.dram_tensor("wdn_s", [DFF, D], BF16).ap()
    dbg_d = None
    if dbg is not None:
        dbg_d = nc.dram_tensor("dbg", [128, dbg[1]], dbg[2], kind="ExternalOutput").ap()

    st = contextlib.ExitStack()
    with st:
        ARENA_N = 106000
        arena_t = st.enter_context(nc.sbuf_tensor("arena", [128, ARENA_N], BF16))
        PS = st.enter_context(nc.psum_tensor("ps", [128, 4096], F32))
        A = Arena(arena_t, ARENA_N)
        S_ = Sched(nc)
        PB = [Buf(f"bank{i}") for i in range(8)]

        def bank(i, n=1):
            return PS[:, i * 512:(i + n) * 512]

        def bankb(i):
            return PS[:, i * 512:(i + 1) * 512].bitcast(BF16)

        def mm(out, lhsT, rhs, start, stop, **kw):
            return S_.pe(lambda e: e.matmul(out, lhsT=lhsT, rhs=rhs, start=start, stop=stop), **kw)

        identb = A.alloc(128, BF16)
        onesb = A.alloc(128, BF16)
        ident32 = A.alloc(128, F32)
        cst = A.alloc(8, F32)
        slp = A.alloc(8, F32)
        gcol = A.alloc(4, F32)
        esink = A.alloc(8, F32)
        convp = A.alloc(4 * 44, F32)
        OTs = A.alloc(4 * S, BF16)
        OTs3 = v3(OTs, 4)
        m_all = A.mark()
        OTM_N = max(4 * S, 16384)
        OTm_full = A.alloc(OTM_N, BF16)
        OTm = OTm_full[:, 0:4 * S]
        OTm3 = v3(OTm, 4)
        m_p3 = A.mark()
        A.release(m_all)
        A.off = m_p3
        cqn = A.alloc(2 * S, BF16)
        ckvn = A.alloc(2 * S, BF16)
        cqn3, ckvn3 = v3(cqn, 2), v3(ckvn, 2)
        krope = A.alloc(S, BF16)
        ropeC = A.alloc(S, BF16)
        ropeS = A.alloc(S, BF16)
        w_qb = A.alloc(2 * 768, BF16)
        w_qbsw = A.alloc(2 * 768, BF16)
        w_kvb = A.alloc(2 * 1024, BF16)
        w_qb3, w_qbsw3, w_kvb3 = v3(w_qb, 2), v3(w_qbsw, 2), v3(w_kvb, 2)
        m_p12 = A.mark()

        B_const = Buf("const")
        dbg_srcs = dict(OTs=OTs, OTm=OTm, cqn=cqn, ckvn=ckvn, krope=krope, ropeC=ropeC, ropeS=ropeS)

        def finish(final_ops):
            finals = list(final_ops)
            if dbg is not None:
                S_.barrier()
                src = dbg_srcs[dbg[0]]
                finals.append(S_.dma("dbg", lambda e: e.dma_start(out=dbg_d, in_=src)))
            S_.emit(final_waits=finals)

        A2 = Arena(OTm_full, OTM_N)
        stg32 = [A2.alloc(512, F32) for _ in range(2)]
        stg16 = [A2.alloc(512, BF16) for _ in range(2)]
        Bstg32 = [Buf() for _ in range(2)]
        Bstg16 = [Buf() for _ in range(2)]
        masks = A2.alloc(2 * 128, F32)
        masks3 = v3(masks, 2)
        dg = A2.alloc(8 * 128, BF16)
        dg3 = v3(dg, 8)
        pcol = A2.alloc(NT, F32)
        npcol = A2.alloc(NT, F32)
        w_in = A2.alloc(8 * IN_W, BF16)
        w_in3 = v3(w_in, 8)
        w_insw = A2.alloc(8 * 96, BF16)
        w_insw3 = v3(w_insw, 8)
        m_p1 = A.mark()

        S_.dma("k1", lambda e: e.dma_start(out=ident32, in_=ident_d), writes=[B_const])
        S_.dma("k2", lambda e: e.dma_start(out=cst, in_=cst_d), writes=[B_const])
        S_.dma("k3", lambda e: e.dma_start(out=slp, in_=slopes_d), writes=[B_const])
        S_.dma("k4", lambda e: e.dma_start(out=masks3, in_=masks_d.rearrange("m p f -> p m f")), writes=[B_const])
        S_.dma("k5", lambda e: e.dma_start(out=esink, in_=sinks_d.partition_broadcast(128)), writes=[B_const])
        S_.dve(lambda e: e.tensor_copy(out=identb, in_=ident32), reads=[B_const], writes=[B_const])
        S_.dve(lambda e: e.memset(onesb, 1.0), writes=[B_const])
        for g in range(2):
            for slot in range(4):
                par, j = slot // 2, slot % 2
                h = 4 * g + 2 * j + par
                S_.dve(lambda e, h=h, k=g * 4 + slot: e.tensor_scalar(out=dg3[:, k, :], in0=ident32, scalar1=slp[:, h:h + 1], scalar2=None,
                                                                      op0=ALU.mult), reads=[B_const], writes=[B_const])
        S_.act(lambda e: e.activation(out=esink, in_=esink, func=AF.Exp), reads=[B_const], writes=[B_const])

        if stop == 0.1:
            finish([])
            return nc
        tmpA = A.alloc(4 * 128, F32)
        tmpA3 = v3(tmpA, 4)
        tmpI = A.alloc(128, I32)
        tmpF = A.alloc(128, F32)
        Btmp = Buf()
        S_.dma("k6", lambda e: e.dma_start(out=tmpA3[0:44, :, :], in_=convp_d.rearrange("k c p -> c k p")), writes=[Btmp])
        for k in range(4):
            S_.pe(lambda e, k=k: e.transpose(out=bank(0)[:, k * 44:(k + 1) * 44], in_=tmpA3[0:44, k, :], identity=ident32[0:44, 0:44]),
                  reads=[Btmp, B_const], writes=[PB[0]])
        S_.dve(lambda e: e.tensor_copy(out=convp, in_=bank(0)[:, 0:176]), reads=[PB[0]], writes=[B_const])
        tmpG = A.alloc(128, F32)
        BtmpG = Buf()
        S_.dma("k7", lambda e: e.dma_start(out=tmpG[0:4, :], in_=graw_d), writes=[BtmpG])
        S_.pe(lambda e: e.transpose(out=bank(1)[:, 0:4], in_=tmpG[0:4, :], identity=ident32[0:4, 0:4]),
              reads=[BtmpG, B_const], writes=[PB[1]])
        S_.dve(lambda e: e.tensor_copy(out=gcol, in_=bank(1)[:, 0:4]), reads=[PB[1]], writes=[B_const])
        BtmpP = Buf()
        S_.dma("k8", lambda e: e.dma_start(out=tmpI[0:NT, :], in_=posr_d), writes=[BtmpP])
        S_.dve(lambda e: e.tensor_copy(out=tmpF[0:NT, :], in_=tmpI[0:NT, :]), reads=[BtmpP], writes=[BtmpP])
        S_.pe(lambda e: e.transpose(out=bank(2)[:, 0:NT], in_=tmpF[0:NT, :], identity=ident32[0:NT, 0:NT]),
              reads=[BtmpP, B_const], writes=[PB[2]])
        S_.dve(lambda e: e.tensor_copy(out=pcol, in_=bank(2)[:, 0:NT]), reads=[PB[2]], writes=[B_const])
        S_.dve(lambda e: e.tensor_scalar(out=npcol, in0=bank(2)[:, 0:NT], scalar1=-1.0, scalar2=None, op0=ALU.mult), reads=[PB[2]], writes=[B_const])

        if stop == 0.2:
            finish([])
            return nc
        r_i = A.alloc(512, I32)
        r_f = A.alloc(512, F32)
        r_t = A.alloc(512, F32)
        r_u = A.alloc(512, F32)
        r_k = A.alloc(512, I32)
        Br = Buf()
        PR = slice(64, 96)
        for c in range(NCH):
            cs = slice(c * 512, (c + 1) * 512)
            S_.dma("rp", lambda e, cs=cs: e.dma_start(out=r_i[PR, :], in_=pos_d[0:1, cs].partition_broadcast(32)), writes=[Br], eng="pool")
            S_.dve(lambda e: e.tensor_copy(out=r_f[PR, :], in_=r_i[PR, :]), reads=[Br], writes=[Br])
            for which in range(2):
                S_.dve(lambda e, which=which: e.tensor_scalar(out=r_t[PR, :], in0=r_f[PR, :], scalar1=cst[PR, 0:1],
                                                              scalar2=(0.25 if which == 0 else 0.0), op0=ALU.mult, op1=ALU.add),
                       reads=[Br, B_const], writes=[Br])
                S_.dve(lambda e: e.tensor_copy(out=r_k[PR, :], in_=r_t[PR, :]), reads=[Br], writes=[Br])
                S_.dve(lambda e: e.tensor_copy(out=r_u[PR, :], in_=r_k[PR, :]), reads=[Br], writes=[Br])
                S_.dve(lambda e: e.tensor_tensor(out=r_t[PR, :], in0=r_t[PR, :], in1=r_u[PR, :], op=ALU.subtract), reads=[Br], writes=[Br])
                if which == 0:
                    S_.act(lambda e, cs=cs: e.activation(out=ropeC[PR, cs], in_=r_t[PR, :], func=AF.Sin, scale=2.0 * math.pi),
                           reads=[Br], writes=[Br])
                else:
                    S_.act(lambda e, cs=cs: e.activation(out=ropeS[PR, cs], in_=r_t[PR, :], func=AF.Sin, scale=cst[PR, 1:2]),
                           reads=[Br, B_const], writes=[Br])

        if stop == 0.3:
            finish([])
            return nc
        cast_i = [0]

        def load_cast(dst_ap_fn, src_ap, ncols, dst_buf, stg=None, Bstg=None):
            stg = stg32 if stg is None else stg
            Bstg = Bstg32 if Bstg is None else Bstg
            for c0 in range(0, ncols, 512):
                c1 = min(ncols, c0 + 512)
                i = cast_i[0] % 2
                cast_i[0] += 1
                S_.dma(f"wl{i}", lambda e, i=i, c0=c0, c1=c1: e.dma_start(out=stg[i][:, 0:c1 - c0], in_=src_ap[:, c0:c1]),
                       writes=[Bstg[i]])
                if i == 0:
                    S_.dve(lambda e, i=i, c0=c0, c1=c1: e.tensor_copy(out=dst_ap_fn(c0, c1), in_=stg[i][:, 0:c1 - c0]),
                           reads=[Bstg[i]], writes=[dst_buf])
                else:
                    S_.act(lambda e, i=i, c0=c0, c1=c1: e.copy(out=dst_ap_fn(c0, c1), in_=stg[i][:, 0:c1 - c0]),
                           reads=[Bstg[i]], writes=[dst_buf])

        Bw = Buf("weights")
        for kc in range(8):
            load_cast(lambda c0, c1, kc=kc: w_in3[:, kc, c0:c1], w_in_d[kc * 128:(kc + 1) * 128, :], IN_W, Bw)
            load_cast(lambda c0, c1, kc=kc: w_insw3[:, kc, c0:c1], w_insw_d[kc * 128:(kc + 1) * 128, :], 96, Bw)
        for kc in range(2):
            load_cast(lambda c0, c1, kc=kc: w_qb3[:, kc, c0:c1], w_qb_d[kc * 128:(kc + 1) * 128, :], 768, Bw)
            load_cast(lambda c0, c1, kc=kc: w_qbsw3[:, kc, c0:c1], w_qbsw_d[kc * 128:(kc + 1) * 128, :], 768, Bw)
            load_cast(lambda c0, c1, kc=kc: w_kvb3[:, kc, c0:c1], w_kvb_d[kc * 128:(kc + 1) * 128, :], 1024, Bw)

        if stop == 0:
            finish([])
            return nc
        S_.barrier()
        A.release(m_p1)
        xt = [A.alloc(1024, F32) for _ in range(2)]
        Bxt = [Buf() for _ in range(2)]
        xb = [A.alloc(1024, BF16) for _ in range(2)]
        Bxb = [Buf() for _ in range(2)]
        xT = A.alloc(8 * 512, BF16)
        xT3 = v3(xT, 8)
        BxT = Buf()
        c32 = A.alloc(2 * 512, F32)
        c323 = v3(c32, 2)
        Bc32 = Buf()
        sq = A.alloc(2 * 512, BF16)
        sq3 = v3(sq, 2)
        Bsq = Buf()
        rr = A.alloc(512, F32)
        Brr = Buf()
        kt1 = A.alloc(512, F32)
        kt2 = A.alloc(512, F32)
        Bkt = Buf()
        ks2 = A.alloc(2 * 1536, BF16)
        ks23 = v3(ks2, 2)
        vs = A.alloc(12 * 256, BF16)
        vs4 = vs.rearrange("p (t g d) -> p t g d", t=12, g=2)
        qz = A.alloc(4096, BF16)
        qz5 = qz.rearrange("p (g b s q) -> p g b s q", g=2, b=4, s=4)
        Bks = [Buf() for _ in range(3)]
        Bvs = [Buf() for _ in range(3)]
        Bqs = [Buf()] * 2
        pq_i = A.alloc(768, I32)
        pq_f = A.alloc(768, F32)
        Bpq = Buf()
        Dt = [A.alloc(3 * 128, BF16) for _ in range(2)]
        Dt3 = [v3(d, 3) for d in Dt]
        BDt = [Buf() for _ in range(2)]
        dtmp = A.alloc(128, F32)
        Bdtmp = Buf()
        Psb = [A.alloc(3 * 512, BF16) for _ in range(2)]
        Psb3 = [v3(p, 3) for p in Psb]
        BPsb = [Buf() for _ in range(2)]
        rec = [A.alloc(512, F32) for _ in range(2)]
        Brec = [Buf() for _ in range(2)]

        S_.pool(lambda e: e.memset(vs, 1.0), writes=Bvs)
        S_.pool(lambda e: e.memset(qz, 0.0), writes=[Bqs[0]])
        esrow = A.alloc(1024, BF16)
        zo = A.alloc(128, BF16)
        ntmp = A.alloc(128, F32)
        S_.dve(lambda e: e.memset(zo[0:1, 0:64], 0.0), writes=[B_const])
        S_.dve(lambda e: e.memset(zo[0:1, 64:128], 1.0), writes=[B_const])
        for g in range(2):
            for slot in range(4):
                par, j = slot // 2, slot % 2
                h = 4 * g + 2 * j + par
                S_.dve(lambda e, g=g, slot=slot, h=h: e.tensor_copy(out=esrow[0:1, g * 512 + slot * 128:g * 512 + (slot + 1) * 128],
                                                                    in_=esink[0:1, h:h + 1].to_broadcast([1, 128])),
                       reads=[B_const], writes=[B_const])

        ev_i = [0]

        act_only = [True]

        def evac(out, in_, reads, writes):
            ev_i[0] += 1
            if act_only[0] or ev_i[0] % 2:
                return S_.act(lambda e: e.copy(out=out, in_=in_), reads=reads, writes=writes)
            return S_.dve(lambda e: e.tensor_copy(out=out, in_=in_), reads=reads, writes=writes)

        pbank = [0]

        def nextbank():
            pbank[0] = (pbank[0] + 1) % 3
            return pbank[0]

        def swa_units(c):
            units_ = []
            cs0 = c * 512

            lo = max(0, cs0 - 128)
            hi = min(S, cs0 + 640)

            def load_pq():
                S_.dma("pq", lambda e: e.dma_start(out=pq_i[:, 0:hi - lo], in_=pos_d[0:1, lo:hi].partition_broadcast(128)), writes=[Bpq])
                S_.dve(lambda e: e.tensor_copy(out=pq_f[:, 0:hi - lo], in_=pq_i[:, 0:hi - lo]), reads=[Bpq], writes=[Bpq])

            for qb in range(4):
                i = c * 4 + qb
                offs = [o for o in (-1, 0, 1) if 0 <= i + o < NT]
                no = len(offs)
                di = i % 2
                qcol = (c % 2) * 512 + qb * 128

                def dtiles(i=i, qb=qb, offs=offs, di=di):
                    for n, o in enumerate(offs):
                        kb = i + o
                        pks = pq_f[:, kb * 128 - lo:kb * 128 - lo + 128]
                        if o == 0:
                            S_.act(lambda e, n=n, pks=pks: e.activation(out=Dt3[di][:, n, :], in_=pks, func=AF.Abs, bias=npcol[:, i:i + 1]),
                                   reads=[Bpq, B_const], writes=[BDt[di]])
                        else:
                            mi = 1 if o == -1 else 0
                            S_.act(lambda e, pks=pks: e.activation(out=dtmp, in_=pks, func=AF.Abs, bias=npcol[:, i:i + 1]),
                                   reads=[Bpq, B_const], writes=[Bdtmp])
                            S_.pool(lambda e, n=n, mi=mi: e.tensor_tensor(out=Dt3[di][:, n, :], in0=dtmp, in1=masks3[:, mi, :], op=ALU.add),
                                    reads=[Bdtmp, B_const], writes=[BDt[di]])

                for g in range(2):
                    pi = (i * 2 + g) % 2
                    pvb = (7, 3)[(i * 2 + g) % 2]

                    def A_(i=i, g=g, qb=qb, offs=offs, no=no, di=di, qcol=qcol, pi=pi, first=(qb == 0 and g == 0), dt=(dtiles if g == 0 else None)):
                        if first:
                            load_pq()
                        if dt is not None:
                            dt()
                        for n, o in enumerate(offs):
                            kb = i + o
                            kcol = (kb % 12) * 128
                            kring = (kb // 4) % 3
                            mm(bank(4 + n), ks23[:, g, kcol:kcol + 128], qz5[:, g, qb, :, :].rearrange("p s q -> p (s q)"), True, False,
                               reads=[Bks[kring], Bqs[0]], writes=[PB[4 + n]])
                            mm(bank(4 + n), Dt3[di][:, n, :], dg[:, g * 512:(g + 1) * 512], False, True,
                               reads=[BDt[di], B_const], writes=[PB[4 + n]])
                        S_.act(lambda e: e.activation(out=Psb[pi][:, 0:no * 512], in_=bank(4, no), func=AF.Exp, scale=0.125),
                               reads=[PB[4 + n] for n in range(no)], writes=[BPsb[pi]])

                    def B_(i=i, g=g, offs=offs, no=no, pi=pi, pvb=pvb):
                        for n, o in enumerate(offs):
                            kb = i + o
                            mm(bank(pvb), vs4[:, kb % 12, g, :], Psb3[pi][:, n, :], n == 0, n == no - 1,
                               reads=[Bvs[(kb // 4) % 3], BPsb[pi]], writes=[PB[pvb]])
                            if n == 0:
                                mm(bank(pvb), zo[0:1, :], esrow[0:1, g * 512:(g + 1) * 512], False, False,
                                   reads=[B_const], writes=[PB[pvb]])
                        S_.act(lambda e: e.activation(out=rec[pi][0:64, :], in_=bank(pvb)[64:128, :], func=AF.Ln),
                               reads=[PB[pvb]], writes=[Brec[pi]])
                        S_.act(lambda e: e.activation(out=rec[pi][0:64, :], in_=rec[pi][0:64, :], func=AF.Exp, scale=-1.0),
                               reads=[Brec[pi]], writes=[Brec[pi]])
                        for par in range(2):
                            S_.dve(lambda e, par=par: e.tensor_tensor(
                                out=OTs3[par * 64:par * 64 + 64, 2 * g:2 * g + 2, i * 128:(i + 1) * 128],
                                in0=v3(bank(pvb)[0:64, par * 256:(par + 1) * 256], 2),
                                in1=v3(rec[pi][0:64, par * 256:(par + 1) * 256], 2), op=ALU.mult),
                                reads=[PB[pvb], Brec[pi]])

                    units_.append((A_, B_))
            return units_

        def latent_pieces(c):
            cs = slice(c * 512, (c + 1) * 512)
            pcs = []
            for which, (col0, dst3, gc) in enumerate(((0, cqn3, 0), (256, ckvn3, 2))):
                def lat(m, col0=col0):
                    b = nextbank()
                    for kc in range(8):
                        mm(bank(b), w_in3[:, kc, col0 + m * 128:col0 + (m + 1) * 128], xT3[:, kc, :], kc == 0, kc == 7,
                           reads=[BxT, Bw], writes=[PB[b]])
                    S_.act(lambda e: e.copy(out=c323[:, m, :], in_=bank(b)), reads=[PB[b]], writes=[Bc32])
                    S_.pool(lambda e: e.tensor_tensor(out=sq3[:, m, :], in0=c323[:, m, :], in1=c323[:, m, :], op=ALU.mult),
                            reads=[Bc32], writes=[Bsq])

                def norm(dst3=dst3, gc=gc):
                    b = nextbank()
                    for m in range(2):
                        mm(bank(b), onesb, sq3[:, m, :], m == 0, m == 1, reads=[Bsq, B_const], writes=[PB[b]])
                    S_.act(lambda e: e.activation(out=rr, in_=bank(b), func=AF.Ln, scale=1.0 / 256.0, bias=cst[:, 2:3]),
                           reads=[PB[b], B_const], writes=[Brr])
                    S_.act(lambda e: e.activation(out=rr, in_=rr, func=AF.Exp, scale=-0.5), reads=[Brr], writes=[Brr])
                    for m in range(2):
                        S_.dve(lambda e, m=m: e.scalar_tensor_tensor(
                            out=dst3[:, m, cs], in0=c323[:, m, :], scalar=gcol[:, gc + m:gc + m + 1], in1=rr, op0=ALU.mult, op1=ALU.mult),
                            reads=[Bc32, Brr, B_const])

                pcs.append(lambda lat=lat: lat(0))
                pcs.append(lambda lat=lat: lat(1))
                pcs.append(norm)

            def kr1():
                b1 = nextbank()
                for kc in range(8):
                    mm(bank(b1)[0:96, :], w_in3[:, kc, 448:544], xT3[:, kc, :], kc == 0, kc == 7, reads=[BxT, Bw], writes=[PB[b1]])
                S_.dve(lambda e: e.tensor_tensor(out=kt1[PR, :], in0=bank(b1)[PR, :], in1=ropeC[PR, cs], op=ALU.mult),
                       reads=[PB[b1], Br], writes=[Bkt])

            def kr2():
                b2 = nextbank()
                for kc in range(8):
                    mm(bank(b2)[0:96, :], w_insw3[:, kc, :], xT3[:, kc, :], kc == 0, kc == 7, reads=[BxT, Bw], writes=[PB[b2]])
                S_.dve(lambda e: e.tensor_tensor(out=kt2[PR, :], in0=bank(b2)[PR, :], in1=ropeS[PR, cs], op=ALU.mult),
                       reads=[PB[b2], Br], writes=[Bkt])
                S_.dve(lambda e: e.tensor_tensor(out=krope[PR, cs], in0=kt1[PR, :], in1=kt2[PR, :], op=ALU.add), reads=[Bkt])

            pcs.append(kr1)
            pcs.append(kr2)
            qpcs = []
            for p in range(4):
                def qsp(p=p):
                    b = nextbank()
                    for kc in range(8):
                        mm(bank(b), w_in3[:, kc, 544 + p * 128:544 + (p + 1) * 128], xT3[:, kc, :], kc == 0, kc == 7,
                           reads=[BxT, Bw], writes=[PB[b]])
                    g_, j_ = p // 2, p % 2
                    evac(qz5[0:64, g_, :, j_, :], v3(bank(b)[0:64, :], 4), [PB[b]], [Bqs[0]])
                    evac(qz5[64:128, g_, :, 2 + j_, :], v3(bank(b)[64:128, :], 4), [PB[b]], [Bqs[0]])
                qpcs.append(qsp)
            return pcs, qpcs

        for c in range(NCH):
            cs = slice(c * 512, (c + 1) * 512)
            for t in range(4):
                tt = c * 4 + t
                i = tt % 2
                S_.dma(f"x{i}", lambda e, i=i, tt=tt: e.dma_start(out=xt[i], in_=x_d[tt * 128:(tt + 1) * 128, :]), writes=[Bxt[i]])
                if t % 2 == 0:
                    S_.dve(lambda e, i=i: e.tensor_copy(out=xb[i], in_=xt[i]), reads=[Bxt[i]], writes=[Bxb[i]])
                else:
                    S_.act(lambda e, i=i: e.copy(out=xb[i], in_=xt[i]), reads=[Bxt[i]], writes=[Bxb[i]])
                for half in range(2):
                    b = nextbank()
                    for k4 in range(4):
                        kc = half * 4 + k4
                        S_.pe(lambda e, b=b, k4=k4, kc=kc, i=i: e.transpose(out=bankb(b)[:, k4 * 128:(k4 + 1) * 128],
                                                                             in_=xb[i][:, kc * 128:(kc + 1) * 128], identity=identb),
                              reads=[Bxb[i], B_const], writes=[PB[b]])
                    evac(xT3[:, half * 4:half * 4 + 4, t * 128:(t + 1) * 128], v3(bankb(b)[:, 0:512], 4), [PB[b]], [BxT])
            b = nextbank()
            for kc in range(8):
                mm(bank(b), w_in3[:, kc, 1056:1184], xT3[:, kc, :], kc == 0, kc == 7, reads=[BxT, Bw], writes=[PB[b]])
            kr = c % 3
            kc0 = kr * 512
            for g in range(2):
                for dsth in range(2):
                    evac(ks23[dsth * 64:dsth * 64 + 64, g, kc0:kc0 + 512], bank(b)[g * 64:g * 64 + 64, :], [PB[b]], [Bks[kr]])
            b = nextbank()
            for t in range(4):
                for kc in range(8):
                    mm(bank(b)[:, t * 128:(t + 1) * 128], xT3[:, kc, t * 128:(t + 1) * 128], w_in3[:, kc, 1184:1312], kc == 0, kc == 7,
                       reads=[BxT, Bw], writes=[PB[b]])
            t0 = (c * 4) % 12
            evac(vs4[:, t0:t0 + 4, :, 0:64], bank(b).rearrange("p (t g d) -> p t g d", t=4, g=2), [PB[b]], [Bvs[kr]])
            pcs, qpcs = latent_pieces(c)
            us = swa_units(c - 1) if c >= 1 else []
            if not us:
                for p in pcs:
                    p()
            else:
                per = -(-len(pcs) // len(us))
                for k, (A_, B_) in enumerate(us):
                    A_()
                    for p in pcs[k * per:(k + 1) * per]:
                        p()
                    B_()
                for p in pcs[len(us) * per:]:
                    p()
            for p in qpcs:
                p()
        for (A_, B_) in swa_units(NCH - 1):
            A_()
            B_()

        if stop == 1:
            finish([])
            return nc
        act_only[0] = False
        S_.barrier()
        A.release(m_p12)
        KT = [A.alloc(S, BF16) for _ in range(2)]
        BKT = [Buf() for _ in range(2)]
        Vh = [A.alloc(NT * 128, BF16) for _ in range(2)]
        Vh3 = [v3(v, NT) for v in Vh]
        BVh = [Buf() for _ in range(2)]
        QT = [A.alloc(512, BF16) for _ in range(2)]
        BQT = [Buf() for _ in range(2)]
        PT = [A.alloc(3 * 512, BF16) for _ in range(3)]
        PT3 = [v3(p, 3) for p in PT]
        BPT = [Buf() for _ in range(3)]
        rec2 = [A.alloc(512, F32) for _ in range(2)]
        Brec2 = [Buf() for _ in range(2)]
        qt1 = A.alloc(512, F32)
        qt2 = A.alloc(512, F32)
        Bqt = Buf()
        for i in range(2):
            S_.pool(lambda e, i=i: e.memset(Vh[i], 1.0), writes=[BVh[i]])
        SC = 96.0 ** -0.5
        NSTG = 4
        sg32 = [A.alloc(1024, F32) for _ in range(NSTG)]
        sg16 = [A.alloc(1024, BF16) for _ in range(NSTG)]
        Bsg32 = [Buf() for _ in range(NSTG)]
        Bsg16 = [Buf() for _ in range(NSTG)]
        scr_jobs = []
        for kc in range(8):
            for hf in range(2):
                for j0 in range(0, NJ, 8):
                    scr_jobs.append(("up", kc, hf, j0, min(8, NJ - j0)))
        for j in range(NJ):
            scr_jobs.append(("dn", j, 0, 0, 0))
        scr_state = [0, 0]

        def emit_scratch(n):
            for _ in range(n):
                if scr_state[0] >= len(scr_jobs):
                    return
                kind, a0, hf, j0, nj = scr_jobs[scr_state[0]]
                scr_state[0] += 1
                i = scr_state[1] % NSTG
                scr_state[1] += 1
                if kind == "up":
                    kc = a0
                    w = nj * 128
                    c0 = hf * DFF + j0 * 128
                    S_.dma(f"wl{i}", lambda e, i=i, kc=kc, c0=c0, w=w: e.dma_start(out=sg32[i][:, 0:w], in_=w_up_d[kc * 128:(kc + 1) * 128, c0:c0 + w]),
                           writes=[Bsg32[i]])
                    S_.dve(lambda e, i=i, w=w: e.tensor_copy(out=sg16[i][:, 0:w], in_=sg32[i][:, 0:w]), reads=[Bsg32[i]], writes=[Bsg16[i]])
                    S_.dma(f"ws{i}", lambda e, i=i, kc=kc, hf=hf, j0=j0, nj=nj, w=w: e.dma_start(
                        out=wup_s[j0:j0 + nj, :, kc, hf * 128:(hf + 1) * 128].rearrange("j p f -> p j f"), in_=v3(sg16[i][:, 0:w], nj)),
                        reads=[Bsg16[i]])
                else:
                    j = a0
                    S_.dma(f"wl{i}", lambda e, i=i, j=j: e.dma_start(out=sg32[i], in_=w_dn_d[j * 128:(j + 1) * 128, :]), writes=[Bsg32[i]])
                    S_.dve(lambda e, i=i: e.tensor_copy(out=sg16[i], in_=sg32[i]), reads=[Bsg32[i]], writes=[Bsg16[i]])
                    S_.dma(f"ws{i}", lambda e, i=i, j=j: e.dma_start(out=wdn_s[j * 128:(j + 1) * 128, :], in_=sg16[i]), reads=[Bsg16[i]])

        groups = []
        kb = 0
        while kb < NT:
            n = min(3, NT - kb)
            groups.append((kb, n))
            kb += n
        NG = len(groups)

        def devac(out, in_, reads, writes):
            return S_.dve(lambda e: e.tensor_copy(out=out, in_=in_), reads=reads, writes=writes)

        def head_pieces(h, pb):
            hb = h % 2
            pcs = []

            def kpiece(c):
                cs = slice(c * 512, (c + 1) * 512)
                for kc in range(2):
                    mm(bank(pb)[0:64, :], w_kvb3[:, kc, h * 128:h * 128 + 64], ckvn3[:, kc, cs], kc == 0, kc == 1, writes=[PB[pb]])
                devac(KT[hb][0:64, cs], bank(pb)[0:64, :], [PB[pb]], [BKT[hb]])

            def vpiece(t8):
                for t in range(t8, t8 + 8):
                    for kc in range(2):
                        mm(bank(pb)[:, (t - t8) * 64:(t - t8 + 1) * 64], ckvn3[:, kc, t * 128:(t + 1) * 128],
                           w_kvb3[:, kc, h * 128 + 64:h * 128 + 128], kc == 0, kc == 1, writes=[PB[pb]])
                devac(Vh3[hb][:, t8:t8 + 8, 0:64], v3(bank(pb), 8), [PB[pb]], [BVh[hb]])

            pcs.append(lambda: S_.pool(lambda e: e.tensor_copy(out=KT[hb][PR, :], in_=krope[PR, :]), writes=[BKT[hb]]))
            for c in range(NCH):
                pcs.append(lambda c=c: kpiece(c))
            for t8 in range(0, NT, 8):
                pcs.append(lambda t8=t8: vpiece(t8))
            return pcs

        def q_pieces(h, c, pb):
            cs = slice(c * 512, (c + 1) * 512)
            qi = (h * NCH + c) % 2

            def q1():
                for kc in range(2):
                    mm(bank(pb)[0:96, :], w_qb3[:, kc, h * 96:(h + 1) * 96], cqn3[:, kc, cs], kc == 0, kc == 1, writes=[PB[pb]])
                S_.dve(lambda e: e.tensor_copy(out=QT[qi][0:64, :], in_=bank(pb)[0:64, :]), reads=[PB[pb]], writes=[BQT[qi]])
                S_.dve(lambda e: e.tensor_tensor(out=qt1[PR, :], in0=bank(pb)[PR, :], in1=ropeC[PR, cs], op=ALU.mult),
                       reads=[PB[pb]], writes=[Bqt])

            def q2():
                for kc in range(2):
                    mm(bank(pb)[0:96, :], w_qbsw3[:, kc, h * 96:(h + 1) * 96], cqn3[:, kc, cs], kc == 0, kc == 1, writes=[PB[pb]])
                S_.dve(lambda e: e.tensor_tensor(out=qt2[PR, :], in0=bank(pb)[PR, :], in1=ropeS[PR, cs], op=ALU.mult),
                       reads=[PB[pb]], writes=[Bqt])
                S_.dve(lambda e: e.tensor_tensor(out=QT[qi][PR, :], in0=qt1[PR, :], in1=qt2[PR, :], op=ALU.add),
                       reads=[Bqt], writes=[BQT[qi]])

            return [q1, q2]

        units = [(h, c, g) for h in range(8) for c in range(NCH) for g in range(NG)]

        def u_info(idx):
            h, c, g = units[idx]
            kb0, n = groups[g]
            return h, c, g, kb0, n, (idx % 2) * 3, idx % 3, (h * NCH + c) % 2, h % 2

        def emit_qk(idx):
            h, c, g, kb0, n, sb0, pi, qi, hb = u_info(idx)
            for j in range(n):
                kbj = kb0 + j
                mm(bank(sb0 + j), KT[hb][0:96, kbj * 128:(kbj + 1) * 128], QT[qi][0:96, :], True, True,
                   reads=[BKT[hb], BQT[qi]], writes=[PB[sb0 + j]])

        def emit_exp(idx):
            h, c, g, kb0, n, sb0, pi, qi, hb = u_info(idx)
            S_.act(lambda e: e.activation(out=PT[pi][:, 0:n * 512], in_=bank(sb0, n), func=AF.Exp, scale=SC),
                   reads=[PB[sb0 + j] for j in range(n)], writes=[BPT[pi]])

        def emit_pv(idx):
            h, c, g, kb0, n, sb0, pi, qi, hb = u_info(idx)
            ab = 6 + (h * NCH + c) % 2
            for j in range(n):
                kbj = kb0 + j
                mm(bank(ab), Vh3[hb][:, kbj, :], PT3[pi][:, j, :], kbj == 0, kbj == NT - 1,
                   reads=[BVh[hb], BPT[pi]], writes=[PB[ab]])

        def emit_norm(h, c):
            cs = slice(c * 512, (c + 1) * 512)
            m = h * NCH + c
            ab = 6 + m % 2
            ri = m % 2
            par = h % 2
            S_.dve(lambda e: e.reciprocal(out=rec2[ri][0:64, :], in_=bank(ab)[64:128, :]), reads=[PB[ab]], writes=[Brec2[ri]])
            S_.dve(lambda e: e.tensor_tensor(out=OTm3[par * 64:par * 64 + 64, h // 2, cs], in0=bank(ab)[0:64, :],
                                             in1=rec2[ri][0:64, :], op=ALU.mult), reads=[PB[ab], Brec2[ri]])

        hp6, hp7 = head_pieces(0, 6), head_pieces(0, 7)
        for k in range(len(hp6)):
            (hp6 if k % 2 == 0 else hp7)[k]()
        for p in q_pieces(0, 0, 6):
            p()
        emit_qk(0)
        emit_qk(1)
        sched = {}
        for m in range(8 * NCH):
            h, c = m // NCH, m % NCH
            u0 = m * NG
            ob = 6 + (m + 1) % 2
            pcs = []
            if c == NCH - 1 and h + 1 < 8:
                pcs += head_pieces(h + 1, ob)
            if m + 1 < 8 * NCH:
                q1, q2 = q_pieces((m + 1) // NCH, (m + 1) % NCH, ob)
                qslot = (max(0, min(NG - 3, 3)), max(0, min(NG - 3, 6)))
            else:
                q1 = q2 = None
            g0 = min(2, max(0, NG - 3))
            L = max(1, NG - 2 - g0)
            per = -(-len(pcs) // L) if pcs else 0
            for g in range(NG):
                lst = []
                if q1 is not None and g == qslot[0]:
                    lst.append(q1)
                k = g - g0
                if per and 0 <= k < L:
                    lst += pcs[k * per:(k + 1) * per] if k < L - 1 else pcs[k * per:]
                if q2 is not None and g == qslot[1]:
                    lst.append(q2)
                sched[u0 + g] = lst
        for idx in range(len(units)):
            h, c, g = units[idx]
            for p in sched.get(idx, []):
                p()
            if g == 5 and not (h == 0 and c == 0):
                emit_scratch(2)
            emit_exp(idx)
            if idx + 2 < len(units):
                emit_qk(idx + 2)
            emit_pv(idx)
            if g == NG - 1:
                emit_norm(h, c)
        emit_scratch(10 ** 6)

        if stop == 2:
            finish([])
            return nc
        S_.barrier()
        A.release(m_p3)
        lnp = A.alloc(4 * D, F32)
        lnp3 = v3(lnp, 4)
        w_o = A.alloc(8 * D, BF16)
        w_o3 = v3(w_o, 8)
        xres = [A.alloc(D, F32) for _ in range(2)]
        Bxres = [Buf() for _ in range(2)]
        NX1 = 9
        x1 = [A.alloc(D, F32) for _ in range(NX1)]
        Bx1 = [Buf() for _ in range(NX1)]
        x1T = A.alloc(8 * 1024, BF16)
        x1T3 = v3(x1T, 8)
        Bx1T = [Buf() for _ in range(8)]
        AT = A.alloc(NJ * 512, BF16)
        AT3 = v3(AT, NJ)
        BAT = [Buf() for _ in range(NJ)]
        wub = [A.alloc(8 * 256, BF16) for _ in range(2)]
        wub3 = [v3(w, 8) for w in wub]
        Bwub = [Buf() for _ in range(2)]
        NWD = 4
        wdb = [A.alloc(D, BF16) for _ in range(NWD)]
        Bwdb = [Buf() for _ in range(NWD)]
        hacc = [A.alloc(512, F32) for _ in range(4)]
        Bhacc = [Buf() for _ in range(4)]
        stats = [A.alloc(32, F32) for _ in range(4)]
        Bstats = [Buf() for _ in range(4)]
        ln_i = [0]
        hbuf = A.alloc(16, BF16)
        hbuf3 = v3(hbuf, 8)
        Bhb = Buf()
        Blnp = Buf()
        for k in range(4):
            S_.dma("c3", lambda e, k=k: e.dma_start(out=lnp3[:, k, :], in_=ln_d[k:k + 1, :].partition_broadcast(128)), writes=[Blnp])
        Bwo = Buf()
        stg3 = [hacc[0], hacc[1]]
        Bstg3 = [Bhacc[0], Bhacc[1]]
        for kc in range(8):
            load_cast(lambda c0, c1, kc=kc: w_o3[:, kc, c0:c1], w_o_d[kc * 128:(kc + 1) * 128, :], D, Bwo, stg3, Bstg3)

        def layer_norm(buf, Bb, gk):
            k = ln_i[0] % 4
            ln_i[0] += 1
            stat, Bstat = stats[k], Bstats[k]
            for hh in range(2):
                S_.dve(lambda e, hh=hh: e.bn_stats(out=stat[:, hh * 6:(hh + 1) * 6], in_=buf[:, hh * 512:(hh + 1) * 512]),
                       reads=[Bb], writes=[Bstat])
            S_.dve(lambda e: e.bn_aggr(out=stat[:, 16:18], in_=stat[:, 0:12]), reads=[Bstat], writes=[Bstat])
            S_.act(lambda e: e.activation(out=stat[:, 18:19], in_=stat[:, 17:18], func=AF.Sqrt, bias=cst[:, 3:4]),
                   reads=[Bstat, B_const], writes=[Bstat])
            S_.dve(lambda e: e.reciprocal(out=stat[:, 19:20], in_=stat[:, 18:19]), reads=[Bstat], writes=[Bstat])
            S_.dve(lambda e: e.scalar_tensor_tensor(out=buf, in0=buf, scalar=stat[:, 16:17], in1=lnp3[:, gk, :], op0=ALU.subtract, op1=ALU.mult),
                   reads=[Bb, Bstat, Blnp], writes=[Bb])
            return S_.dve(lambda e: e.scalar_tensor_tensor(out=buf, in0=buf, scalar=stat[:, 19:20], in1=lnp3[:, gk + 1, :], op0=ALU.mult, op1=ALU.add),
                          reads=[Bb, Bstat, Blnp], writes=[Bb])

        def post_a1(t, b0):
            xi = t % 2
            s = t % NX1
            S_.dma(f"x{xi}", lambda e: e.dma_start(out=xres[xi], in_=x_d[t * 128:(t + 1) * 128, :]), writes=[Bxres[xi]])
            for nh in range(2):
                for kp in range(8):
                    src = OTm3[:, kp, t * 128:(t + 1) * 128] if kp < 4 else OTs3[:, kp - 4, t * 128:(t + 1) * 128]
                    mm(bank(b0 + nh), src, w_o3[:, kp, nh * 512:(nh + 1) * 512], kp == 0, kp == 7, reads=[Bwo], writes=[PB[b0 + nh]])
                S_.dve(lambda e, nh=nh: e.scalar_tensor_tensor(out=x1[s][:, nh * 512:(nh + 1) * 512], in0=xres[xi][:, nh * 512:(nh + 1) * 512],
                                                               scalar=ALPHA, in1=bank(b0 + nh), op0=ALU.mult, op1=ALU.add),
                       reads=[Bxres[xi], PB[b0 + nh]], writes=[Bx1[s]])

        def post_a1_ln(t):
            s = t % NX1
            layer_norm(x1[s], Bx1[s], 0)

        def post_a2(t, b0):
            s = t % NX1
            rc = (t % 8) * 128
            for half in range(2):
                b = b0 + half
                for k4 in range(4):
                    kc = half * 4 + k4
                    S_.pe(lambda e, b=b, k4=k4, kc=kc: e.transpose(out=bank(b)[:, k4 * 128:(k4 + 1) * 128], in_=x1[s][:, kc * 128:(kc + 1) * 128],
                                                                   identity=ident32), reads=[Bx1[s], B_const], writes=[PB[b]])
                S_.act(lambda e, b=b, half=half: e.copy(out=x1T3[:, half * 4:half * 4 + 4, rc:rc + 128], in_=v3(bank(b), 4)),
                       reads=[PB[b]], writes=[Bx1T[t % 8]])

        out_ops = []
        deferred = []

        def ffn(c, mid=None, post=None):
            base = (c % 2) * 512
            right = (base + 512) % 1024
            rd = [Bx1T[(4 * c + k) % 8] for k in range(4)]
            rdh = [Bx1T[(4 * c - 1) % 8], Bx1T[(4 * c + 4) % 8]]
            has_l, has_r = c > 0, c < NCH - 1

            def load_dn(j):
                i = (c * NJ + j) % NWD
                S_.dma(f"wd{i}", lambda e: e.dma_start(out=wdb[i], in_=wdn_s[j * 128:(j + 1) * 128, :]), writes=[Bwdb[i]])

            def load_up(j):
                i = (c * NJ + j) % 2
                S_.dma(f"wu{i}", lambda e: e.dma_start(out=wub3[i], in_=wup_s[j]), writes=[Bwub[i]])

            load_up(0)
            left = (base - 1) % 1024
            S_.pool(lambda e: e.tensor_copy(out=hbuf3[:, :, 0:1], in_=x1T3[:, :, right:right + 1]), reads=rdh, writes=[Bhb])
            S_.pool(lambda e: e.tensor_copy(out=hbuf3[:, :, 1:2], in_=x1T3[:, :, left:left + 1]), reads=rdh, writes=[Bhb])
            for j in range(NJ):
                if j + 1 < NJ:
                    load_up(j + 1)
                if j == NJ - 4:
                    for jj in range(NWD - 1):
                        load_dn(jj)
                if j == 6:
                    while deferred:
                        deferred.pop(0)()
                wi = (c * NJ + j) % 2
                accs = []
                for half in range(2):
                    f = half * NJ + j
                    hb_ = (j * 2 + half) % 6
                    xb_ = 6 + half
                    ai = (j * 2 + half) % 4
                    for kc in range(8):
                        mm(bank(hb_), wub3[wi][:, kc, half * 128:(half + 1) * 128], x1T3[:, kc, base:base + 512], kc == 0, kc == 7,
                           reads=[Bwub[wi]] + rd, writes=[PB[hb_]])
                    if has_l or has_r:
                        for kc in range(8):
                            mm(bank(xb_)[:, 0:2], wub3[wi][:, kc, half * 128:(half + 1) * 128],
                               hbuf3[:, kc, :], kc == 0, kc == 7, reads=[Bwub[wi], Bhb], writes=[PB[xb_]])
                    w0 = convp[:, 0 * 44 + f:0 * 44 + f + 1]
                    w1 = convp[:, 1 * 44 + f:1 * 44 + f + 1]
                    w2 = convp[:, 2 * 44 + f:2 * 44 + f + 1]
                    bb = convp[:, 3 * 44 + f:3 * 44 + f + 1]
                    acc = hacc[ai]
                    S_.act(lambda e, acc=acc, hb_=hb_, w1=w1, bb=bb: e.activation(out=acc, in_=bank(hb_), func=AF.Identity, scale=w1, bias=bb),
                           reads=[PB[hb_], B_const], writes=[Bhacc[ai]])
                    S_.dve(lambda e, acc=acc, hb_=hb_, w0=w0: e.scalar_tensor_tensor(out=acc[:, 1:512], in0=bank(hb_)[:, 0:511], scalar=w0,
                                                                                     in1=acc[:, 1:512], op0=ALU.mult, op1=ALU.add),
                           reads=[PB[hb_], Bhacc[ai], B_const], writes=[Bhacc[ai]])
                    S_.dve(lambda e, acc=acc, hb_=hb_, w2=w2: e.scalar_tensor_tensor(out=acc[:, 0:511], in0=bank(hb_)[:, 1:512], scalar=w2,
                                                                                     in1=acc[:, 0:511], op0=ALU.mult, op1=ALU.add),
                           reads=[PB[hb_], Bhacc[ai], B_const], writes=[Bhacc[ai]])
                    if has_l:
                        S_.dve(lambda e, acc=acc, xb_=xb_, w0=w0: e.scalar_tensor_tensor(out=acc[:, 0:1], in0=bank(xb_)[:, 1:2], scalar=w0,
                                                                                         in1=acc[:, 0:1], op0=ALU.mult, op1=ALU.add),
                               reads=[PB[xb_], Bhacc[ai], B_const], writes=[Bhacc[ai]])
                    if has_r:
                        S_.dve(lambda e, acc=acc, xb_=xb_, w2=w2: e.scalar_tensor_tensor(out=acc[:, 511:512], in0=bank(xb_)[:, 0:1], scalar=w2,
                                                                                         in1=acc[:, 511:512], op0=ALU.mult, op1=ALU.add),
                               reads=[PB[xb_], Bhacc[ai], B_const], writes=[Bhacc[ai]])
                    accs.append((acc, Bhacc[ai]))
                (ag, Bg), (au, Bu) = accs
                S_.act(lambda e, ag=ag: e.activation(out=ag, in_=ag, func=AF.Gelu), reads=[Bg], writes=[Bg])
                S_.pool(lambda e, ag=ag, au=au, j=j: e.tensor_tensor(out=AT3[:, j, :], in0=ag, in1=au, op=ALU.mult),
                        reads=[Bg, Bu], writes=[BAT[j]])
            if mid is not None:
                mid()
            for j in range(NJ):
                if j + NWD - 1 < NJ:
                    load_dn(j + NWD - 1)
                wi = (c * NJ + j) % NWD
                for tl in range(4):
                    for nh in range(2):
                        b = tl * 2 + nh
                        mm(bank(b), AT3[:, j, tl * 128:(tl + 1) * 128], wdb[wi][:, nh * 512:(nh + 1) * 512], j == 0, j == NJ - 1,
                           reads=[BAT[j], Bwdb[wi]], writes=[PB[b]])
            for tl in range(4):
                t = c * 4 + tl
                s = t % NX1
                for nh in range(2):
                    b = tl * 2 + nh
                    S_.dve(lambda e, s=s, nh=nh, b=b: e.scalar_tensor_tensor(out=x1[s][:, nh * 512:(nh + 1) * 512],
                                                                             in0=x1[s][:, nh * 512:(nh + 1) * 512], scalar=ALPHA,
                                                                             in1=bank(b), op0=ALU.mult, op1=ALU.add),
                           reads=[PB[b], Bx1[s]], writes=[Bx1[s]])
            if post is not None:
                post()
            for tl in range(4):
                t = c * 4 + tl
                s = t % NX1
                layer_norm(x1[s], Bx1[s], 2)
                deferred.append(lambda s=s, t=t: out_ops.append(
                    S_.dma(f"o{t % 4}", lambda e: e.dma_start(out=out_d[t * 128:(t + 1) * 128, :], in_=x1[s]), reads=[Bx1[s]])))

        for k, t in enumerate(range(min(NT, 5))):
            post_a1(t, (k % 4) * 2)
        for k, t in enumerate(range(min(NT, 5))):
            post_a1_ln(t)
        for k, t in enumerate(range(min(NT, 5))):
            post_a2(t, (k % 4) * 2)
        for c in range(NCH):
            nxt = list(range(4 * c + 5, min(NT, 4 * c + 9)))

            def mid(nxt=nxt):
                for k, t in enumerate(nxt):
                    post_a1(t, k * 2)
                for k, t in enumerate(nxt):
                    post_a1_ln(t)

            def post(nxt=nxt):
                for k, t in enumerate(nxt):
                    post_a2(t, k * 2)

            ffn(c, mid, post)

        while deferred:
            deferred.pop(0)()
        finish(out_ops[-4:])
    return nc


def _consts():
    inv_freq = (10000.0 ** (-np.arange(0, 32, 2, dtype=np.float32) / 32.0)).astype(np.float32)
    cst = np.zeros((128, 8), np.float32)
    for p in range(64, 96):
        j = (p - 64) % 16
        cst[p, 0] = inv_freq[j] / (2.0 * math.pi)
        cst[p, 1] = (-1.0 if p < 80 else 1.0) * 2.0 * math.pi
    cst[:, 2] = RMS_EPS
    cst[:, 3] = LN_EPS
    slopes = (2.0 ** (-8.0 * (np.arange(8, dtype=np.float32) + 1.0) / 8.0)).astype(np.float32)
    slp = np.broadcast_to((-8.0 * slopes)[None, :], (128, 8)).astype(np.float32).copy()
    k = np.arange(128)[:, None]
    q = np.arange(128)[None, :]
    masks = np.zeros((2, 128, 128), np.float32)
    masks[0] = np.where(k >= q, 0.0, BIGM)
    masks[1] = np.where(k <= q, 0.0, BIGM)
    return cst, slp, masks, np.eye(128, dtype=np.float32)


def make_in_maps(S, x, positions, w_in, q_norm_g, w_q_b, kv_norm_g, w_kv_b, swa_sinks, w_o, ln1_g, ln1_b, w_up, conv_w,
                 conv_b, w_down, ln2_g, ln2_b):
    f = lambda a: np.ascontiguousarray(np.asarray(a, dtype=np.float32))
    cst, slp, masks, ident = _consts()
    perm = np.concatenate([np.arange(16, 32), np.arange(0, 16)])
    w_in = f(w_in)
    w_insw = np.concatenate([w_in[:, 448:512], w_in[:, 512 + perm]], axis=1)
    w_q_b = f(w_q_b)
    w_qbsw = w_q_b.copy()
    for h in range(8):
        w_qbsw[:, h * 96 + 64:h * 96 + 96] = w_q_b[:, h * 96 + 64 + perm]
    graw = np.stack([f(q_norm_g)[0:128], f(q_norm_g)[128:256], f(kv_norm_g)[0:128], f(kv_norm_g)[128:256]])
    lnp = np.stack([f(ln1_g), f(ln1_b), f(ln2_g), f(ln2_b)])
    cw = f(conv_w).reshape(3, 2 * DFF)
    convp = np.stack([cw[0].reshape(44, 128), cw[1].reshape(44, 128), cw[2].reshape(44, 128), f(conv_b).reshape(44, 128)])
    shared = {
        "w_in": w_in, "w_insw": f(w_insw), "w_qb": w_q_b, "w_qbsw": f(w_qbsw), "w_kvb": f(w_kv_b), "w_o": f(w_o),
        "w_up": f(w_up), "w_down": f(w_down), "graw": f(graw), "sinks": f(swa_sinks).reshape(1, 8), "lnp": f(lnp),
        "convp": f(convp), "ident": ident, "cst": cst, "masks": masks, "slopes": slp,
    }
    x = np.asarray(x, dtype=np.float32)
    positions = np.asarray(positions, dtype=np.int32)
    maps = []
    for b in range(x.shape[0]):
        m = dict(shared)
        m["x"] = np.ascontiguousarray(x[b])
        m["pos"] = np.ascontiguousarray(positions[b].reshape(1, S))
        m["posr"] = np.ascontiguousarray(positions[b].reshape(S // 128, 128))
        maps.append(m)
    return maps


_NC_CACHE = {}


def kernel(**inputs):
    x = np.asarray(inputs["x"])
    B, S, _ = x.shape
    if S not in _NC_CACHE:
        _NC_CACHE[S] = build(S)
    nc = _NC_CACHE[S]
    maps = make_in_maps(S, **inputs)
    res = run_bass_kernel_spmd(nc, maps, core_ids=list(range(B)))
    return np.stack([np.asarray(r["out"], dtype=np.float32) for r in res.results], axis=0)
```
